# Optimizing a Trainium2 kernel written in Bass

```python
import jax
import jax.numpy as jnp
from jax import lax
import numpy as np

D_MODEL = 1024
BATCH = 8
SEQ = 4096
DEPTH = 1
DEC_BATCH = 8
DEC_SEQ = 64
PAST_LEN = 2048

CHUNK = 64
LEFT_CHUNKS = 8
BAND = 512
N_HEADS_SB = 8
N_HEADS_CB = 8
HEAD_DIM = 64
WIDTH_SB = 512
WIDTH_CB = 512
REL_CLIP = 128
Q_BLOCK = 128
EPS = 1e-6
NEG_INF = -1e30
ATTN_SCALE = 0.125
IN_COLS = 6144
SPLIT_POINTS = (512, 1024, 1536, 2048, 2560, 3072, 3584, 4096, 5120)

kernel_name = "hybrid_stickbreak_chunkband_stream_step"


def rms_norm(x, w):
    xf = x.astype(jnp.float32)
    ms = jnp.mean(xf * xf, axis=-1, keepdims=True)
    return (xf * lax.rsqrt(ms + EPS)).astype(x.dtype) * w


def project(x, norm_w, w_in, q_norm_w, k_norm_w):
    h = rms_norm(x, norm_w) @ w_in
    sb_q, sb_k, sb_v, sb_z, cb_q, cb_k, cb_v, cb_z, g_sb, g_cb = jnp.split(h, SPLIT_POINTS, axis=-1)
    lead = x.shape[:-1]
    sb_q = sb_q.reshape(lead + (N_HEADS_SB, HEAD_DIM))
    sb_k = sb_k.reshape(lead + (N_HEADS_SB, HEAD_DIM))
    sb_v = sb_v.reshape(lead + (N_HEADS_SB, HEAD_DIM))
    cb_q = rms_norm(cb_q.reshape(lead + (N_HEADS_CB, HEAD_DIM)), q_norm_w)
    cb_k = rms_norm(cb_k.reshape(lead + (N_HEADS_CB, HEAD_DIM)), k_norm_w)
    cb_v = cb_v.reshape(lead + (N_HEADS_CB, HEAD_DIM))
    return sb_q, sb_k, sb_v, sb_z, cb_q, cb_k, cb_v, cb_z, g_sb, g_cb


def stick_breaking(q, k, v, q_pos, k_pos):
    z = jnp.einsum("bqhd,bkhd->bhqk", q, k).astype(jnp.float32) * ATTN_SCALE
    causal = k_pos[None, :] < q_pos[:, None]
    log_beta = jax.nn.log_sigmoid(z)
    log_keep = jnp.where(causal, jax.nn.log_sigmoid(-z), 0.0)
    suffix = lax.cumsum(log_keep, axis=3, reverse=True) - log_keep
    w = jnp.where(causal, jnp.exp(log_beta + suffix), 0.0)
    return jnp.einsum("bhqk,bkhd->bqhd", w.astype(v.dtype), v)


def band_attention(q, k, v, q_pos, k_pos, rel_bias):
    s = jnp.einsum("...qhd,...khd->...hqk", q, k).astype(jnp.float32) * ATTN_SCALE
    rel = jnp.clip(q_pos[..., :, None] - k_pos[..., None, :], -REL_CLIP, REL_CLIP) + REL_CLIP
    bias = jnp.moveaxis(rel_bias[:, rel], 0, -3).astype(jnp.float32)
    mask = (k_pos >= 0)[..., None, None, :]
    s = jnp.where(mask, s + bias, NEG_INF)
    p = jax.nn.softmax(s, axis=-1).astype(v.dtype)
    return jnp.einsum("...hqk,...khd->...qhd", p, v)


def chunk_band(kv, nc):
    b = kv.shape[0]
    kc = kv.reshape((b, nc, CHUNK) + kv.shape[2:])
    kp = jnp.pad(kc, ((0, 0), (LEFT_CHUNKS, 0), (0, 0), (0, 0), (0, 0)))
    return jnp.concatenate([kp[:, o:o + nc] for o in range(LEFT_CHUNKS + 1)], axis=2)


def merge(x, o_sb, z_sb, o_cb, z_cb, g_sb, g_cb, w_proj_sb, w_proj_cb, w_out):
    lead = x.shape[:-1]
    b_sb = (o_sb.reshape(lead + (WIDTH_SB,)) * jax.nn.silu(z_sb)) @ w_proj_sb
    b_cb = (o_cb.reshape(lead + (WIDTH_CB,)) * jax.nn.silu(z_cb)) @ w_proj_cb
    h = jax.nn.sigmoid(g_sb) * b_sb + jax.nn.sigmoid(g_cb) * b_cb
    return x + h @ w_out


def setup_inputs(seed: int = 0) -> dict:
    key = jax.random.key(seed)
    ks = jax.random.split(key, 14)
    f = jnp.float32
    r = min(BAND, PAST_LEN)

    def nrm(k, shape, scale):
        return jax.random.normal(k, shape, f) * scale

    return {
        "x_prompt": nrm(ks[0], (BATCH, SEQ, D_MODEL), 1.0),
        "x_sample": nrm(ks[1], (DEC_BATCH, DEC_SEQ, D_MODEL), 1.0),
        "cache_sb_k": nrm(ks[2], (DEPTH, DEC_BATCH, PAST_LEN, N_HEADS_SB, HEAD_DIM), 1.0),
        "cache_sb_v": nrm(ks[3], (DEPTH, DEC_BATCH, PAST_LEN, N_HEADS_SB, HEAD_DIM), 1.0),
        "cache_cb_k": nrm(ks[4], (DEPTH, DEC_BATCH, r, N_HEADS_CB, HEAD_DIM), 1.0),
        "cache_cb_v": nrm(ks[5], (DEPTH, DEC_BATCH, r, N_HEADS_CB, HEAD_DIM), 1.0),
        "norm_w": 1.0 + nrm(ks[6], (DEPTH, D_MODEL), 0.05),
        "w_in": nrm(ks[7], (DEPTH, D_MODEL, IN_COLS), D_MODEL ** -0.5),
        "q_norm_w": 1.0 + nrm(ks[8], (DEPTH, HEAD_DIM), 0.05),
        "k_norm_w": 1.0 + nrm(ks[9], (DEPTH, HEAD_DIM), 0.05),
        "rel_bias": nrm(ks[10], (DEPTH, N_HEADS_CB, 2 * REL_CLIP + 1), 0.5),
        "w_proj_sb": nrm(ks[11], (DEPTH, WIDTH_SB, D_MODEL), WIDTH_SB ** -0.5),
        "w_proj_cb": nrm(ks[12], (DEPTH, WIDTH_CB, D_MODEL), WIDTH_CB ** -0.5),
        "w_out": nrm(ks[13], (DEPTH, D_MODEL, D_MODEL), D_MODEL ** -0.5),
    }


def reference(x_prompt, x_sample, cache_sb_k, cache_sb_v, cache_cb_k, cache_cb_v,
              norm_w, w_in, q_norm_w, k_norm_w, rel_bias, w_proj_sb, w_proj_cb, w_out):
    b, t, _ = x_prompt.shape
    tn = x_sample.shape[1]
    p = cache_sb_k.shape[2]
    r = cache_cb_k.shape[2]
    nc = t // CHUNK
    nb = t // Q_BLOCK
    keep = min(BAND, t)

    pos_p = jnp.arange(t)
    qpos_blocks = pos_p.reshape(nb, Q_BLOCK)
    cb_qpos = pos_p.reshape(nc, CHUNK)
    cb_kpos = (jnp.arange(nc)[:, None] - LEFT_CHUNKS) * CHUNK + jnp.arange((LEFT_CHUNKS + 1) * CHUNK)[None, :]
    sb_s_kpos = jnp.arange(p + tn)
    s_qpos = p + jnp.arange(tn)
    cb_s_kpos = p - r + jnp.arange(r + tn)

    y_p, y_s = x_prompt, x_sample
    sb_k_p, sb_v_p, cb_k_p, cb_v_p = [], [], [], []
    sb_k_s, sb_v_s, cb_k_s, cb_v_s = [], [], [], []
    for l in range(DEPTH):
        sq, sk, sv, sz, cq, ck, cv, cz, gs, gc = project(y_p, norm_w[l], w_in[l], q_norm_w[l], k_norm_w[l])
        qb = jnp.moveaxis(sq.reshape(b, nb, Q_BLOCK, N_HEADS_SB, HEAD_DIM), 1, 0)
        o_sb = lax.map(lambda a: stick_breaking(a[0], sk, sv, a[1], pos_p), (qb, qpos_blocks))
        o_sb = jnp.moveaxis(o_sb, 0, 1).reshape(b, t, N_HEADS_SB, HEAD_DIM)
        o_cb = band_attention(cq.reshape(b, nc, CHUNK, N_HEADS_CB, HEAD_DIM),
                              chunk_band(ck, nc), chunk_band(cv, nc),
                              cb_qpos, cb_kpos, rel_bias[l])
        o_cb = o_cb.reshape(b, t, N_HEADS_CB, HEAD_DIM)
        y_p_next = merge(y_p, o_sb, sz, o_cb, cz, gs, gc, w_proj_sb[l], w_proj_cb[l], w_out[l])
        sb_k_p.append(sk)
        sb_v_p.append(sv)
        cb_k_p.append(ck[:, t - keep:])
        cb_v_p.append(cv[:, t - keep:])

        sq2, sk2, sv2, sz2, cq2, ck2, cv2, cz2, gs2, gc2 = project(y_s, norm_w[l], w_in[l], q_norm_w[l], k_norm_w[l])
        k_all = jnp.concatenate([cache_sb_k[l].astype(sk2.dtype), sk2], axis=1)
        v_all = jnp.concatenate([cache_sb_v[l].astype(sv2.dtype), sv2], axis=1)
        o_sb2 = stick_breaking(sq2, k_all, v_all, s_qpos, sb_s_kpos)
        kb = jnp.concatenate([cache_cb_k[l].astype(ck2.dtype), ck2], axis=1)
        vb = jnp.concatenate([cache_cb_v[l].astype(cv2.dtype), cv2], axis=1)
        o_cb2 = band_attention(cq2, kb, vb, s_qpos, cb_s_kpos, rel_bias[l])
        y_s_next = merge(y_s, o_sb2, sz2, o_cb2, cz2, gs2, gc2, w_proj_sb[l], w_proj_cb[l], w_out[l])
        sb_k_s.append(sk2)
        sb_v_s.append(sv2)
        cb_k_s.append(ck2)
        cb_v_s.append(cv2)

        y_p, y_s = y_p_next, y_s_next

    return (y_p, y_s,
            jnp.stack(sb_k_p), jnp.stack(sb_v_p), jnp.stack(cb_k_p), jnp.stack(cb_v_p),
            jnp.stack(sb_k_s), jnp.stack(sb_v_s), jnp.stack(cb_k_s), jnp.stack(cb_v_s))
```

```python
import numpy as np
from contextlib import ExitStack
import concourse.bass as bass
import concourse.mybir as mybir
from concourse.bass_utils import run_bass_kernel_spmd

F32 = mybir.dt.float32
BF16 = mybir.dt.bfloat16
AF = mybir.ActivationFunctionType
ALU = mybir.AluOpType
AX = mybir.AxisListType

T_P = 4096
T_S = 64
TT = T_P + T_S
D = 1024
PAST = 2048
EPS = 1e-6
ENGS = ("pe", "act", "dve", "pool", "sp")
DEBUG = False


class Buf:
    __slots__ = ("w", "r", "name")

    def __init__(self, name=""):
        self.w = []
        self.r = {}
        self.name = name


class Op:
    __slots__ = ("eng", "fn", "deps", "signal", "sem", "val", "isdma", "key")


class Prog:
    def __init__(self, nc, engsem, keysems):
        self.nc = nc
        self.ops = {e: [] for e in ENGS}
        self.engsem = engsem
        self.freekeys = list(keysems)
        self.keysem = {}
        self.keycnt = {}
        self.cnt = {e: 0 for e in ENGS}
        self.lastdma = {}
        self.ndma = 0

    def add(self, eng, fn, reads=(), writes=(), deps=(), key=None):
        op = Op()
        op.eng = eng
        op.fn = fn
        op.signal = False
        op.isdma = key is not None
        op.key = key
        op.sem = None
        op.val = 0
        d = []
        for b in reads:
            d += b.w
        for b in writes:
            d += b.w
            for v in b.r.values():
                d += v
        d += list(deps)
        dd = []
        seen = set()
        for x in d:
            if x is op or id(x) in seen:
                continue
            seen.add(id(x))
            if (not x.isdma) and x.eng == "pe" and eng == "pe":
                continue
            x.signal = True
            dd.append(x)
        op.deps = dd
        for b in reads:
            if op.isdma:
                b.r.setdefault("dma", []).append(op)
            else:
                b.r[eng] = [op]
        for b in writes:
            b.w = [op]
            b.r = {}
        if op.isdma:
            if key not in self.keysem:
                self.keysem[key] = self.freekeys.pop()
                self.keycnt[key] = 0
            self.keycnt[key] += 1
            op.sem = self.keysem[key]
            op.val = 16 * self.keycnt[key]
            self.lastdma[key] = op
            self.ndma += 1
        self.ops[eng].append(op)
        return op

    def emit(self, block):
        for e in ENGS:
            for op in self.ops[e]:
                if (not op.isdma) and op.signal:
                    self.cnt[e] += 1
                    op.sem = self.engsem[e]
                    op.val = self.cnt[e]
        finals = [(self.keysem[k], 16 * self.keycnt[k]) for k in self.keysem]

        def run(e, eng):
            waited = {}
            for op in self.ops[e]:
                need = {}
                for x in op.deps:
                    sid = id(x.sem)
                    if waited.get(sid, 0) >= x.val:
                        continue
                    if sid not in need or need[sid][1] < x.val:
                        need[sid] = (x.sem, x.val)
                need = list(need.values())
                attach = None
                if need and e != "pe":
                    attach = need.pop()
                for (sm, vl) in need:
                    eng.wait_ge(sm, vl)
                    waited[id(sm)] = vl
                ins = op.fn(eng)
                if attach is not None:
                    ins._wait_ge(attach[0], attach[1])
                    waited[id(attach[0])] = attach[1]
                if op.isdma:
                    ins.then_inc(op.sem, 16)
                elif op.signal:
                    ins.then_inc(op.sem, 1)
            if e == "sp":
                for (s, v) in finals:
                    eng.wait_ge(s, v)

        block.tensor(lambda eng: run("pe", eng))
        block.scalar(lambda eng: run("act", eng))
        block.vector(lambda eng: run("dve", eng))
        block.gpsimd(lambda eng: run("pool", eng))
        block.sync(lambda eng: run("sp", eng))


def build_program():
    nc = bass.Bass("TRN2", target_bir_lowering=False)

    def din(name, shape):
        return nc.dram_tensor(name, shape, F32, kind="ExternalInput").ap()

    def dout(name, shape):
        return nc.dram_tensor(name, shape, F32, kind="ExternalOutput").ap()

    x_p = din("x_p", [T_P, D])
    x_s = din("x_s", [T_S, D])
    csk = din("csk", [PAST, 512])
    csv = din("csv", [PAST, 512])
    cck = din("cck", [512, 512])
    ccv = din("ccv", [512, 512])
    norm_w = din("norm_w", [1, D])
    w_in = din("w_in", [D, 6144])
    qnw_d = din("qnw", [1, 64])
    knw_d = din("knw", [1, 64])
    relb = din("relb", [8, 257])
    wps_d = din("wps", [512, D])
    wpc_d = din("wpc", [512, D])
    wo_d = din("wo", [D, D])
    y_p = dout("y_p", [T_P, D])
    y_s = dout("y_s", [T_S, D])
    sbk_p = dout("sbk_p", [T_P, 512])
    sbv_p = dout("sbv_p", [T_P, 512])
    cbk_p = dout("cbk_p", [512, 512])
    cbv_p = dout("cbv_p", [512, 512])
    sbk_s = dout("sbk_s", [T_S, 512])
    sbv_s = dout("sbv_s", [T_S, 512])
    cbk_s = dout("cbk_s", [T_S, 512])
    cbv_s = dout("cbv_s", [T_S, 512])
    skind = dict(kind="ExternalOutput") if DEBUG else {}
    xnT_d = nc.dram_tensor("xnT_d", [128, 8, TT], BF16, **skind).ap()
    og_d = nc.dram_tensor("og_d", [128, 4, TT], BF16, **skind).ap()
    S_d = nc.dram_tensor("S_d", [8, 128, 768], F32).ap()
    EB_d = nc.dram_tensor("EB_d", [128, 8, 640], BF16).ap()

    def x_rows(tok0, R):
        if tok0 >= T_P:
            return x_s[tok0 - T_P:tok0 - T_P + R, :]
        return x_p[tok0:tok0 + R, :]

    def y_rows(tok0, R, c0, cn):
        if tok0 >= T_P:
            return y_s[tok0 - T_P:tok0 - T_P + R, c0:c0 + cn]
        return y_p[tok0:tok0 + R, c0:c0 + cn]

    with ExitStack() as es:
        ARENA_COLS = 53200
        arena = es.enter_context(nc.sbuf_tensor("arena", [128, ARENA_COLS], F32))
        psum = es.enter_context(nc.psum_tensor("psum", [128, 4096], F32))
        engsem = {}
        for e in ("pe", "act", "dve", "pool"):
            engsem[e] = es.enter_context(nc.semaphore("s_" + e))
        keysems = [es.enter_context(nc.semaphore("k%d" % i)) for i in range(80)]
        P = Prog(nc, engsem, keysems)

        off = [0]

        def sb(free_shape, dtype):
            n = 1
            for s in free_shape:
                n *= s
            nb = n * (2 if dtype == BF16 else 4)
            ncol = (nb + 3) // 4
            ap = arena[:, off[0]:off[0] + ncol]
            off[0] += ncol
            assert off[0] <= ARENA_COLS, ("sbuf overflow", off[0])
            if dtype == BF16:
                ap = ap.bitcast(BF16)
            if len(free_shape) == 2:
                ap = ap.rearrange("p (a b) -> p a b", a=free_shape[0])
            elif len(free_shape) == 3:
                ap = ap.rearrange("p (a b c) -> p a b c", a=free_shape[0], b=free_shape[1])
            return ap

        def bank(i, n=1):
            return psum[:, i * 512:(i + n) * 512]

        onesf = sb([128], F32)
        nonesf = sb([128], F32)
        ident = sb([128], BF16)
        nTriI = sb([128], BF16)
        nTriC = sb([128], BF16)
        blk64 = sb([128], BF16)
        onesb = sb([128], BF16)
        normw_bc = sb([D], F32)
        qnw8 = sb([1], F32)
        knw1 = sb([1], F32)
        small = sb([64], F32)
        CONST_END = off[0]
        cbuf = Buf("consts")

        o = P.add("pool", lambda g: g.memset(onesf, 1.0), writes=[cbuf])
        P.add("pool", lambda g: g.memset(nonesf, -1.0), writes=[cbuf])
        P.add("pool", lambda g: g.memset(onesb, 1.0), writes=[cbuf])
        P.add("pool", lambda g: g.affine_select(out=ident, in_=onesf, pattern=[[-1, 128]], compare_op=ALU.is_equal,
                                                fill=0.0, base=0, channel_multiplier=1), reads=[cbuf], writes=[cbuf])
        P.add("pool", lambda g: g.affine_select(out=nTriI, in_=nonesf, pattern=[[-1, 128]], compare_op=ALU.is_ge,
                                                fill=0.0, base=0, channel_multiplier=1), reads=[cbuf], writes=[cbuf])
        P.add("pool", lambda g: g.affine_select(out=nTriC, in_=nonesf, pattern=[[1, 128]], compare_op=ALU.is_gt,
                                                fill=0.0, base=0, channel_multiplier=-1), reads=[cbuf], writes=[cbuf])
        P.add("pool", lambda g: g.memset(blk64, 0.0), writes=[cbuf])
        P.add("pool", lambda g: g.memset(blk64[0:64, 0:64], 1.0 / 64), writes=[cbuf])
        P.add("pool", lambda g: g.memset(blk64[64:128, 64:128], 1.0 / 64), writes=[cbuf])
        cdmas = [P.add("sp", lambda q: q.dma_start(out=normw_bc, in_=bass.AP(norm_w.tensor, 0, [[0, 128], [1, D]])), key="c0")]
        for hh in range(2):
            cdmas.append(P.add("sp", lambda q, hh=hh: q.dma_start(out=qnw8[64 * hh:64 * hh + 64, :],
                                                                  in_=bass.AP(qnw_d.tensor, 0, [[1, 64], [1, 1]])), key="c0"))
            cdmas.append(P.add("sp", lambda q, hh=hh: q.dma_start(out=knw1[64 * hh:64 * hh + 64, :],
                                                                  in_=bass.AP(knw_d.tensor, 0, [[1, 64], [1, 1]])), key="c0"))
        cbuf.w = cbuf.w + cdmas
        P.add("dve", lambda v: v.tensor_scalar(qnw8, qnw8, 0.125, None, ALU.mult), reads=[cbuf], writes=[cbuf])

        NBLK = 9
        xnTd_buf = [Buf("xnTd%d" % i) for i in range(NBLK)]
        ogd_buf = [Buf("ogd%d" % i) for i in range(NBLK)]

        def blk_tok0(I):
            return I * 512

        def blk_ntok(I):
            return 64 if I == 8 else 512

        def slot_of(I):
            return 0 if I == 8 else (I + 1) % 2

        off[0] = CONST_END
        w1 = sb([8, 2048], BF16)
        xt = [sb([D], F32) for _ in range(2)]
        xnb = [sb([D], BF16) for _ in range(2)]
        xnT = [sb([8, 512], BF16) for _ in range(2)]
        stg = [sb([512], F32) for _ in range(4)]
        wst = [sb([1024], F32) for _ in range(2)]
        DEAD_END = off[0]
        KT = sb([4, 4224], BF16)
        Vsb = sb([33, 512], BF16)
        QT = [sb([4, 512], BF16) for _ in range(2)]
        sZ = [sb([4, 512], BF16) for _ in range(2)]
        og = [sb([4, 512], BF16) for _ in range(2)]
        ebuf = [sb([2, 512], F32) for _ in range(3)]
        spb = [sb([2, 512], BF16) for _ in range(3)]
        Peb = [sb([2, 512], F32) for _ in range(2)]
        Wb = [sb([2, 512], BF16) for _ in range(2)]
        PH1_END = off[0]

        b_w1 = Buf("w1")
        b_KT = [[Buf("KT%d_%d" % (hp, j)) for j in range(33)] for hp in range(4)]
        b_V = [Buf("V%d" % j) for j in range(33)]
        b_xt = [Buf() for _ in range(2)]
        b_xnb = [Buf() for _ in range(2)]
        b_xnT = [Buf() for _ in range(2)]
        b_QT = [[Buf() for _ in range(4)] for _ in range(2)]
        b_sZ = [[Buf() for _ in range(4)] for _ in range(2)]
        b_og = [Buf() for _ in range(2)]
        b_e = [Buf() for _ in range(3)]
        b_Pe = [Buf() for _ in range(2)]
        b_sp = [Buf() for _ in range(3)]
        b_W = [Buf() for _ in range(2)]
        b_stg = [Buf() for _ in range(4)]
        b_wst = [Buf() for _ in range(2)]
        b_small = Buf()
        Zp = psum[:, 0:1024].rearrange("p (h n) -> p h n", h=2)
        Rp = psum[:, 1024:2048].rearrange("p (h n) -> p h n", h=2)
        Op_ = [bank(4), bank(5)]
        PJ = [bank(6), bank(7)]
        b_Z = Buf("Z")
        b_R = Buf("R")
        b_O = [Buf("O0"), Buf("O1")]
        b_PJ = [Buf("PJ0"), Buf("PJ1")]
        cnt = {"pj": 0, "stg": 0, "wst": 0, "xt": 0}

        def next_pj():
            i = cnt["pj"] % 2
            cnt["pj"] += 1
            return i

        def next_stg():
            i = cnt["stg"] % 4
            cnt["stg"] += 1
            return i

        stage_slots = [(wst[0], b_wst[0], "wst0"), (wst[1], b_wst[1], "wst1"), (xt[0], b_xt[0], "xt0"), (xt[1], b_xt[1], "xt1")]

        def next_stage():
            i = cnt["wst"] % 4
            cnt["wst"] += 1
            return stage_slots[i]

        def load_w_chunk(src_ap, dst_ap, dst_buf):
            st_ap, st_buf, st_key = [stage_slots[0], stage_slots[1], stage_slots[3]][cnt["wst"] % 3]
            cnt["wst"] += 1
            ncol = src_ap.shape[1]
            eng = "dve"
            P.add("sp", lambda q: q.dma_start(out=st_ap[:, 0:ncol], in_=src_ap), writes=[st_buf], key=st_key)
            P.add(eng, lambda v: v.tensor_copy(out=dst_ap, in_=st_ap[:, 0:ncol]), reads=[st_buf], writes=[dst_buf])

        b_w1c = [Buf("w1c%d" % g_) for g_ in range(4)]

        def load_w1():
            for g_ in range(4):
                for kc in range(8):
                    load_w_chunk(w_in[kc * 128:(kc + 1) * 128, g_ * 512:(g_ + 1) * 512],
                                 w1[:, kc, g_ * 512:(g_ + 1) * 512], b_w1c[g_])

        P.add("pool", lambda g: g.memset(KT[:, :, 4096:4224], 0.0), writes=[b_KT[hp][32] for hp in range(4)])
        P.add("pool", lambda g: g.memset(Vsb[:, 32, :], 0.0), writes=[b_V[32]])
        PTb = PJ[0].bitcast(BF16).rearrange("p (a b) -> p a b", a=8)
        PTb1 = PJ[1].bitcast(BF16).rearrange("p (a b) -> p a b", a=8)
        b_Sd = Buf("Sd")

        def startup_late():
            for kt in range(15, -1, -1):
                st_ap, st_buf, st_key = next_stage()
                P.add("sp", lambda q, st_ap=st_ap, kt=kt: q.dma_start(out=st_ap[:, 0:512], in_=csk[kt * 128:(kt + 1) * 128, :]),
                      writes=[st_buf], key=st_key)
                xs = cnt["xt"] % 2
                cnt["xt"] += 1
                P.add("dve", lambda v, st_ap=st_ap, xs=xs: v.tensor_copy(out=xnb[xs][:, 0:512], in_=st_ap[:, 0:512]),
                      reads=[st_buf], writes=[b_xnb[xs]])
                pj = next_pj()
                ptv = PTb if pj == 0 else PTb1
                for hp in range(4):
                    P.add("pe", lambda t, hp=hp, xs=xs, ptv=ptv: t.transpose(out=ptv[:, hp, :], in_=xnb[xs][:, hp * 128:(hp + 1) * 128],
                                                                              identity=ident),
                          reads=[b_xnb[xs], cbuf], writes=[b_PJ[pj]] if hp == 0 else [], deps=[])
                lastT = P.ops["pe"][-1]
                b_PJ[pj].w = [lastT]
                P.add("dve", lambda v, kt=kt, ptv=ptv: v.tensor_copy(out=KT[:, :, (16 + kt) * 128:(17 + kt) * 128], in_=ptv[:, 0:4, :]),
                      reads=[b_PJ[pj]], writes=[b_KT[hp][16 + kt] for hp in range(4)])
                st2_ap, st2_buf, st2_key = next_stage()
                P.add("sp", lambda q, st2_ap=st2_ap, kt=kt: q.dma_start(out=st2_ap[:, 0:512], in_=csv[kt * 128:(kt + 1) * 128, :]),
                      writes=[st2_buf], key=st2_key)
                P.add("act", lambda a, st2_ap=st2_ap, kt=kt: a.activation(out=Vsb[:, 16 + kt, :], in_=st2_ap[:, 0:512], func=AF.Copy),
                      reads=[st2_buf], writes=[b_V[16 + kt]])

        RB1 = sb([768], F32)
        b_RB1 = Buf("RB1")
        def rb_chain_gen():
            for h in range(8):
                P.add("sp", lambda q, h=h: q.dma_start(out=RB1[:, 0:257], in_=bass.AP(relb.tensor, h * 257, [[0, 128], [1, 257]])),
                      writes=[b_RB1], key="rb")
                yield
                yield
                yield
                P.add("pool", lambda g: g.memset(RB1[:, 257:768], 0.0), reads=[b_RB1], writes=[b_RB1])
                P.add("dve", lambda v: v.tensor_scalar(RB1[:, 257:768], RB1[:, 257:768], RB1[:, 256:257], None, ALU.add),
                      reads=[b_RB1], writes=[b_RB1])
                yield
                P.add("sp", lambda q, h=h: q.dma_start(out=S_d[h], in_=RB1), reads=[b_RB1], writes=[b_Sd] if h == 0 else [], key="rb2")
                if h != 0:
                    b_Sd.w = b_Sd.w + [P.ops["sp"][-1]]


                yield

        b_smallt = [Buf() for _ in range(4)]

        prefetched = {}

        def proj1(I, nxt=None):
            slot = slot_of(I)
            tok0 = blk_tok0(I)
            ntok = blk_ntok(I)
            ntile = (ntok + 127) // 128
            sample = (I == 8)
            tinfo = []
            for t in range(ntile):
                R = min(128, ntok - t * 128)
                tinfo.append(dict(t=t, R=R, r0=tok0 + t * 128, ss=small[0:R, 4 * t:4 * t + 1], ms=small[0:R, 4 * t + 1:4 * t + 2],
                                  ln=small[0:R, 4 * t + 2:4 * t + 3], rs=small[0:R, 4 * t + 3:4 * t + 4], bs=b_smallt[t]))

            def stA(ti):
                ti["xs"] = cnt["xt"] % 2
                cnt["xt"] += 1
                xs, r0, R = ti["xs"], ti["r0"], ti["R"]
                P.add("sp", lambda q: q.dma_start(out=xt[xs][0:R, :], in_=x_rows(r0, R)), writes=[b_xt[xs]], key="xt%d" % xs)

            def stB(ti):
                xs, R, ss, ms, bs = ti["xs"], ti["R"], ti["ss"], ti["ms"], ti["bs"]
                P.add("dve", lambda v: v.scalar_tensor_tensor(out=xnb[xs][0:R, :], in0=xt[xs][0:R, :], scalar=1.0, in1=xt[xs][0:R, :],
                                                              op0=ALU.mult, op1=ALU.mult, accum_out=ss),
                      reads=[b_xt[xs]], writes=[b_xnb[xs], bs])
                P.add("dve", lambda v: v.tensor_scalar(ms, ss, 1.0 / D, EPS, ALU.mult, ALU.add), reads=[bs], writes=[bs])

            def stC(ti):
                ms, ln, rs, bs = ti["ms"], ti["ln"], ti["rs"], ti["bs"]
                P.add("act", lambda a: a.activation(out=ln, in_=ms, func=AF.Ln), reads=[bs], writes=[bs])
                P.add("act", lambda a: a.activation(out=rs, in_=ln, func=AF.Exp, scale=-0.5), reads=[bs], writes=[bs])

            def stD(ti):
                xs, R, rs, bs = ti["xs"], ti["R"], ti["rs"], ti["bs"]
                P.add("dve", lambda v: v.scalar_tensor_tensor(out=xnb[xs][0:R, :], in0=xt[xs][0:R, :], scalar=rs, in1=normw_bc[0:R, :],
                                                              op0=ALU.mult, op1=ALU.mult),
                      reads=[b_xt[xs], bs, cbuf], writes=[b_xnb[xs]])

            def stE(ti):
                xs, R = ti["xs"], ti["R"]
                pj = next_pj()
                ti["pj"] = pj
                ptv = PTb if pj == 0 else PTb1
                for kc in range(8):
                    P.add("pe", lambda t_, kc=kc: t_.transpose(out=ptv[:, kc, 0:R], in_=xnb[xs][0:R, kc * 128:(kc + 1) * 128],
                                                                identity=ident[0:R, 0:R]),
                          reads=[b_xnb[xs], cbuf], writes=[b_PJ[pj]] if kc == 0 else [])
                b_PJ[pj].w = [P.ops["pe"][-1]]

            def stF(ti):
                t, R, pj = ti["t"], ti["R"], ti["pj"]
                ptv = PTb if pj == 0 else PTb1
                P.add("dve", lambda v: v.tensor_copy(out=xnT[slot][:, :, t * 128:t * 128 + R], in_=ptv[:, :, 0:R]),
                      reads=[b_PJ[pj]], writes=[b_xnT[slot]])

            pre = prefetched.get(I, None)
            if pre is not None:
                for t in range(min(2, ntile)):
                    tinfo[t]["xs"] = pre[t]
            base = 0 if pre is not None else 3
            sched = {}

            def at(k, fn, ti):
                sched.setdefault(k, []).append((fn, ti))

            for t in range(ntile):
                ti = tinfo[t]
                if t < 2:
                    if pre is None:
                        at(0, stA, ti)
                    b0 = base + 2 * t
                else:
                    b0 = base + 2 * (t - 2) + 16
                for off_, fn in zip([0, 3, 5, 6, 8], [stB, stC, stD, stE, stF]):
                    at(b0 + off_, fn, ti)
                if t + 2 < ntile:
                    at(b0 + 5, stA, tinfo[t + 2])
            for k in range(max(sched) + 1):
                for (fn, ti) in sched.get(k, []):
                    fn(ti)
                yield
            P.add("sp", lambda q: q.dma_start(out=xnT_d[:, :, tok0:tok0 + ntok], in_=xnT[slot][:, :, 0:ntok]),
                  reads=[b_xnT[slot]], writes=[xnTd_buf[I]], key="xs%d" % slot)
            yield

            def fm_mm(c0, pj):
                for kc in range(8):
                    P.add("pe", lambda t_, kc=kc: t_.matmul(PJ[pj][:, 0:ntok], lhsT=w1[:, kc, c0:c0 + 128], rhs=xnT[slot][:, kc, 0:ntok],
                                                              start=(kc == 0), stop=(kc == 7)),
                          reads=[b_w1c[c0 // 512], b_xnT[slot]], writes=[b_PJ[pj]] if kc == 0 else [])
                    b_PJ[pj].w = [P.ops["pe"][-1]]
                    if kc % 2 == 1:
                        yield

            def tm_mm(t, R, c0, pj):
                for kc in range(8):
                    P.add("pe", lambda t_, kc=kc: t_.matmul(PJ[pj][0:R, :], lhsT=xnT[slot][:, kc, t * 128:t * 128 + R],
                                                              rhs=w1[:, kc, c0:c0 + 512], start=(kc == 0), stop=(kc == 7)),
                          reads=[b_w1c[c0 // 512], b_xnT[slot]], writes=[b_PJ[pj]] if kc == 0 else [])
                    b_PJ[pj].w = [P.ops["pe"][-1]]
                    if kc % 2 == 1:
                        yield

            def cons_q(hp, pj):
                P.add("dve", lambda v: v.tensor_scalar(QT[slot][:, hp, 0:ntok], PJ[pj][:, 0:ntok], 0.125, None, ALU.mult),
                      reads=[b_PJ[pj]], writes=[b_QT[slot][hp]])

            def cons_k(hp, pj):
                kcol = 4096 if sample else tok0
                kb_list = [b_KT[hp][32]] if sample else [b_KT[hp][tok0 // 128 + j] for j in range(4)]
                P.add("dve", lambda v: v.tensor_copy(out=KT[:, hp, kcol:kcol + ntok], in_=PJ[pj][:, 0:ntok]),
                      reads=[b_PJ[pj]], writes=kb_list)

            def cons_z(hp, pj):
                si = next_stg()
                P.add("act", lambda a: a.activation(out=stg[si][:, 0:ntok], in_=PJ[pj][:, 0:ntok], func=AF.Exp, scale=-1.0),
                      reads=[b_PJ[pj]], writes=[b_stg[si]])
                P.add("act", lambda a: a.activation(out=stg[si][:, 0:ntok], in_=stg[si][:, 0:ntok], func=AF.Ln, bias=1.0),
                      reads=[b_stg[si]], writes=[b_stg[si]])
                P.add("act", lambda a: a.activation(out=stg[si][:, 0:ntok], in_=stg[si][:, 0:ntok], func=AF.Exp, scale=-1.0),
                      reads=[b_stg[si]], writes=[b_stg[si]])
                P.add("dve", lambda v: v.tensor_tensor(out=sZ[slot][:, hp, 0:ntok], in0=PJ[pj][:, 0:ntok], in1=stg[si][:, 0:ntok], op=ALU.mult),
                      reads=[b_stg[si], b_PJ[pj]], writes=[b_sZ[slot][hp]])

            def cons_tm(t, R, which, pj):
                si = next_stg()
                P.add("dve", lambda v: v.tensor_copy(out=stg[si][0:R, :], in_=PJ[pj][0:R, :]), reads=[b_PJ[pj]], writes=[b_stg[si]])
                if sample:
                    dst = (sbk_s if which == 0 else sbv_s)[0:R, :]
                else:
                    dst = (sbk_p if which == 0 else sbv_p)[tok0 + t * 128:tok0 + t * 128 + R, :]
                if which == 1:
                    vt = 32 if sample else (tok0 // 128 + t)
                    P.add("pool", lambda g: g.tensor_copy(out=Vsb[0:R, vt, :], in_=stg[si][0:R, :]), reads=[b_stg[si]], writes=[b_V[vt]])
                P.add("sp", lambda q: q.dma_start(out=dst, in_=stg[si][0:R, :]), reads=[b_stg[si]], key="st%d" % si)

            groups = []
            for hp in range(4):
                groups.append((lambda pj, hp=hp: fm_mm(hp * 128, pj), lambda pj, hp=hp: cons_q(hp, pj)))
                groups.append((lambda pj, hp=hp: fm_mm(512 + hp * 128, pj), lambda pj, hp=hp: cons_k(hp, pj)))
                groups.append((lambda pj, hp=hp: fm_mm(1536 + hp * 128, pj), lambda pj, hp=hp: cons_z(hp, pj)))
            for t in range(ntile):
                R = min(128, ntok - t * 128)
                for which in range(2):
                    c0 = 512 if which == 0 else 1024
                    groups.append((lambda pj, t=t, R=R, c0=c0: tm_mm(t, R, c0, pj), lambda pj, t=t, R=R, which=which: cons_tm(t, R, which, pj)))
            pending = None
            for (mmf, consf) in groups:
                pj = next_pj()
                yield from mmf(pj)
                if pending is not None:
                    pending()
                    yield
                pending = (lambda consf=consf, pj=pj: consf(pj))
            pending()
            yield
            if nxt is not None:
                nt_n = (blk_ntok(nxt) + 127) // 128
                lst = []
                for t in range(min(2, nt_n)):
                    R = min(128, blk_ntok(nxt) - t * 128)
                    ti = dict(R=R, r0=blk_tok0(nxt) + t * 128)
                    stA(ti)
                    lst.append(ti["xs"])
                prefetched[nxt] = lst
                yield

        def make_steps(I):
            slot = slot_of(I)
            N = blk_ntok(I)
            steps = []
            if I == 8:
                for idx, J in enumerate(range(16, -1, -1)):
                    st = dict(I=I, hp=0, J=16 + J, slot=slot, N=256, first=(idx == 0), last=(idx == 16), all=True,
                              diag=(0 if J == 16 else None), c0=0)
                    steps.append(st)
                return steps
            for hp in range(4):
                if I == 8:
                    Js = list(range(16, -1, -1))
                else:
                    Js = list(range(4 * I + 3, -1, -1))
                for idx, J in enumerate(Js):
                    st = dict(I=I, hp=hp, J=(16 + J if I == 8 else J), slot=slot, N=N, first=(idx == 0), last=(idx == len(Js) - 1))
                    if I == 8:
                        st["diag"] = 0 if J == 16 else None
                    else:
                        st["diag"] = (J - 4 * I) * 128 if J >= 4 * I else None
                    st["c0"] = st["diag"] if st["diag"] is not None else 0
                    steps.append(st)
            return steps

        gcount = {"n": 0, "o": 0}

        def addZ(s):
            hp, J, N, slot, c0 = s["hp"], s["J"], s["N"], s["slot"], s["c0"]
            if s.get("all"):
                first = True
                for hp_ in range(4):
                    for h in range(2):
                        P.add("pe", lambda t, h=h, hp_=hp_: t.matmul(Zp[:, h, hp_ * 64:(hp_ + 1) * 64], lhsT=KT[64 * h:64 * h + 64, hp_, J * 128:(J + 1) * 128],
                                                                     rhs=QT[slot][64 * h:64 * h + 64, hp_, 0:64], start=True, stop=True),
                              reads=[b_KT[q_][J] for q_ in range(4)] + [b_QT[slot][q_] for q_ in range(4)] if first else [],
                              writes=[b_Z] if first else [])
                        first = False
                b_Z.w = [P.ops["pe"][-1]]
                return
            for h in range(2):
                P.add("pe", lambda t, h=h: t.matmul(Zp[:, h, c0:N], lhsT=KT[64 * h:64 * h + 64, hp, J * 128:(J + 1) * 128],
                                                     rhs=QT[slot][64 * h:64 * h + 64, hp, c0:N], start=True, stop=True),
                      reads=[b_KT[hp][J], b_QT[slot][hp]], writes=[b_Z] if h == 0 else [])
            b_Z.w = [P.ops["pe"][-1]]

        def addExpLn(s):
            n = s["n"]
            N, c0 = s["N"], s["c0"]
            e = ebuf[n % 3]
            spx = spb[n % 3]
            b_e[n % 3].r = {}
            P.add("act", lambda a: a.activation(out=e[:, :, c0:N], in_=Zp[:, :, c0:N], func=AF.Exp), reads=[b_Z], writes=[b_e[n % 3]])
            P.add("act", lambda a: a.activation(out=spx[:, :, c0:N], in_=e[:, :, c0:N], func=AF.Ln, bias=1.0),
                  reads=[b_e[n % 3]], writes=[b_sp[n % 3]])
            if s["diag"] is not None:
                base = c0 - s["diag"]
                pat = [[0, 2], [0, 4], [1, 64]] if s.get("all") else [[0, 2], [1, N - c0]]
                P.add("pool", lambda g: g.affine_select(out=spx[:, :, c0:N], in_=spx[:, :, c0:N], pattern=pat,
                                                        compare_op=ALU.is_gt, fill=0.0, base=base, channel_multiplier=-1),
                      reads=[b_sp[n % 3]], writes=[b_sp[n % 3]])

        def addRupd(s, prev):
            n, N, c0 = s["n"], s["N"], s["c0"]
            spx = spb[n % 3]
            first = s["first"]
            ops = []
            if not first:
                spp = spb[prev["n"] % 3]
                pc0 = prev["c0"]
                for h in range(2):
                    ops.append(lambda t, h=h: t.matmul(Rp[:, h, pc0:N], lhsT=nTriC, rhs=spp[:, h, pc0:N], start=False, stop=True,
                                                       skip_group_check=True))
            for h in range(2):
                ops.append(lambda t, h=h: t.matmul(Rp[:, h, c0:N], lhsT=nTriI, rhs=spx[:, h, c0:N], start=first, stop=True,
                                                   skip_group_check=True))
            rd = [b_sp[n % 3], cbuf]
            if not first:
                rd += [b_sp[prev["n"] % 3]]
            for i, fn in enumerate(ops):
                P.add("pe", fn, reads=rd if i == 0 else [], writes=[b_R] if i == 0 else [])
            b_R.w = [P.ops["pe"][-1]]

        def addExpR(s):
            n, hp, J, N, slot, I, c0 = s["n"], s["hp"], s["J"], s["N"], s["slot"], s["I"], s["c0"]
            W = Wb[n % 2]
            Pe = Peb[n % 2]
            e = ebuf[n % 3]
            P.add("act", lambda a: a.activation(out=Pe[:, :, c0:N], in_=Rp[:, :, c0:N], func=AF.Exp), reads=[b_R], writes=[b_Pe[n % 2]])
            P.add("dve", lambda v: v.tensor_tensor(out=W[:, :, c0:N], in0=Pe[:, :, c0:N], in1=e[:, :, c0:N], op=ALU.mult),
                  reads=[b_Pe[n % 2], b_e[n % 3]], writes=[b_W[n % 2]])
            if s["diag"] is not None:
                base = c0 - s["diag"]
                pat = [[0, 2], [0, 4], [1, 64]] if s.get("all") else [[0, 2], [1, N - c0]]
                P.add("pool", lambda g: g.affine_select(out=W[:, :, c0:N], in_=W[:, :, c0:N], pattern=pat,
                                                        compare_op=ALU.is_gt, fill=0.0, base=base, channel_multiplier=-1),
                      reads=[b_W[n % 2]], writes=[b_W[n % 2]])

        def addAV(s):
            n, hp, J, N, slot, I, c0 = s["n"], s["hp"], s["J"], s["N"], s["slot"], s["I"], s["c0"]
            W = Wb[n % 2]
            if s["first"]:
                gcount["o"] += 1
            oi = gcount["o"] % 2
            s["oi"] = oi
            if s.get("all"):
                first = True
                for hp_ in range(4):
                    for h in range(2):
                        cc = hp_ * 128 + h * 64
                        P.add("pe", lambda t, h=h, hp_=hp_, cc=cc: t.matmul(Op_[oi][64 * h:64 * h + 64, hp_ * 64:(hp_ + 1) * 64], lhsT=Vsb[:, J, cc:cc + 64],
                                                                            rhs=W[:, h, hp_ * 64:(hp_ + 1) * 64], start=(s["first"] and hp_ == 0),
                                                                            stop=s["last"], skip_group_check=True),
                              reads=[b_W[n % 2], b_V[J]] if first else [], writes=[b_O[oi]] if (first and s["first"]) else [])
                        first = False
                b_O[oi].w = [P.ops["pe"][-1]]
                if s["last"]:
                    P.add("dve", lambda v: v.tensor_tensor(out=og[slot][:, :, 0:64], in0=Op_[oi][:, 0:256].rearrange("p (a b) -> p a b", a=4),
                                                           in1=sZ[slot][:, :, 0:64], op=ALU.mult),
                          reads=[b_O[oi]] + [b_sZ[slot][q_] for q_ in range(4)], writes=[b_og[slot]])
                    tok0 = blk_tok0(I)
                    P.add("sp", lambda q: q.dma_start(out=og_d[:, :, tok0:tok0 + 64], in_=og[slot][:, :, 0:64]),
                          reads=[b_og[slot]], writes=[ogd_buf[I]], key="og%d" % slot)
                return
            for h in range(2):
                cc = hp * 128 + h * 64
                P.add("pe", lambda t, h=h, cc=cc: t.matmul(Op_[oi][64 * h:64 * h + 64, c0:N], lhsT=Vsb[:, J, cc:cc + 64], rhs=W[:, h, c0:N],
                                                            start=s["first"], stop=s["last"], skip_group_check=True),
                      reads=[b_W[n % 2], b_V[J]] if h == 0 else [], writes=[b_O[oi]] if (h == 0 and s["first"]) else [])
            b_O[oi].w = [P.ops["pe"][-1]]
            if s["last"]:
                P.add("dve", lambda v: v.tensor_tensor(out=og[slot][:, hp, 0:N], in0=Op_[oi][:, 0:N], in1=sZ[slot][:, hp, 0:N], op=ALU.mult),
                      reads=[b_O[oi], b_sZ[slot][hp]], writes=[b_og[slot]] if hp == 0 else [])
                if hp != 0:
                    b_og[slot].w = [P.ops["dve"][-1]]
                if hp == 3:
                    tok0 = blk_tok0(I)
                    P.add("sp", lambda q: q.dma_start(out=og_d[:, :, tok0:tok0 + N], in_=og[slot][:, :, 0:N]),
                          reads=[b_og[slot]], writes=[ogd_buf[I]], key="og%d" % slot)

        print('CONST_END', CONST_END, 'DEAD_END', DEAD_END, 'PH1 end cols', off[0])
        _save = off[0]
        off[0] = CONST_END
        w2 = sb([8, 4096], BF16)
        wst2 = [sb([512], F32) for _ in range(2)]
        BTf_e = [sb([640], F32) for _ in range(2)]
        EBs_e = [sb([640], BF16) for _ in range(2)]
        assert off[0] <= DEAD_END, (off[0], DEAD_END)
        PH2_BASE = CONST_END + 8 * 4096 // 2 + 2 * 512
        b_BTe = [Buf(), Buf()]
        b_EBs = [Buf(), Buf()]
        b_EBd = Buf("EBd")
        off[0] = _save
        b_w2 = Buf("w2")
        b_wst2 = [Buf(), Buf()]
        c2 = {"pj": 0, "stg": 0, "wst": 0, "x": 0, "xr": 0, "wsl": 0, "p": 0, "sq": 0, "sg": 0, "t": 0}

        def nx(k, m):
            i = c2[k] % m
            c2[k] += 1
            return i

        wslots = {"list": None}

        def load_w2_chunk(src_ap, dst_ap, dst_buf):
            ncol = src_ap.shape[1]
            if wslots["list"] is None:
                i = nx("wst", 2)
                st_ap, st_buf, st_key = wst2[i], b_wst2[i], "w2s%d" % i
            else:
                st_ap, st_buf, st_key = wslots["list"][nx("wsl", len(wslots["list"]))]
            P.add("sp", lambda q: q.dma_start(out=st_ap[:, 0:ncol], in_=src_ap), writes=[st_buf], key=st_key)
            ceng = "dve" if wslots["list"] is None else "pool"
            P.add(ceng, lambda v: v.tensor_copy(out=dst_ap, in_=st_ap[:, 0:ncol]), reads=[st_buf], writes=[dst_buf])

        def w2_loader():
            early = [P.ops[e][-1] for e in ("pe", "act", "dve", "pool") if P.ops[e]] + list(P.lastdma.values())
            b_w2.w = list(early)
            b_wst2[0].w = list(early)
            b_wst2[1].w = list(early)
            for kc in range(8):
                for q8 in range(8):
                    load_w2_chunk(w_in[kc * 128:(kc + 1) * 128, 2048 + q8 * 512:2048 + (q8 + 1) * 512],
                                  w2[:, kc, q8 * 512:(q8 + 1) * 512], b_w2)
                    yield
            for i_ in range(2):
                b_BTe[i_].w = list(early)
                b_EBs[i_].w = list(early)
            for h in range(8):
                bt, bb, eb, be = BTf_e[h % 2], b_BTe[h % 2], EBs_e[h % 2], b_EBs[h % 2]
                P.add("sp", lambda q, h=h, bt=bt: q.dma_start(out=bt, in_=bass.AP(S_d.tensor, h * 128 * 768 + 128, [[767, 128], [1, 640]])),
                      reads=[b_Sd], writes=[bb], key="rb3%d" % (h % 2))
                P.add("pool", lambda g, bt=bt: g.memset(bt[0:64, 576:640], -30000.0), reads=[bb], writes=[bb])
                P.add("pool", lambda g, bt=bt: g.memset(bt[64:128, 0:64], -30000.0), reads=[bb], writes=[bb])
                P.add("act", lambda a_, bt=bt, eb=eb: a_.activation(out=eb, in_=bt, func=AF.Exp), reads=[bb], writes=[be])
                P.add("sp", lambda q, h=h, eb=eb: q.dma_start(out=EB_d[:, h, :], in_=eb), reads=[be], writes=[b_EBd] if h == 0 else [],
                      key="ebd%d" % (h % 2))
                if h != 0:
                    b_EBd.w = b_EBd.w + [P.ops["sp"][-1]]
                yield

        order = [8] + list(range(8))
        g_first = proj1(order[0], None)
        for _ in range(13):
            next(g_first)
        load_w1()
        for _ in g_first:
            pass
        startup_late()
        all_steps = []
        blk_range = {}
        for I in order:
            st = make_steps(I)
            blk_range[I] = (len(all_steps), len(all_steps) + len(st))
            all_steps += st
        for n, s in enumerate(all_steps):
            s["n"] = n
        gens = {}
        for bi, I in enumerate(order):
            if bi + 1 < len(order):
                gens[I] = proj1(order[bi + 1], order[bi + 2] if bi + 2 < len(order) else None)
        gens[order[-1]] = w2_loader()

        def chain_gens(*gs):
            for g_ in gs:
                yield from g_

        gens[2] = chain_gens(gens[2], rb_chain_gen())
        PROJ_OPS_EST = 140
        addZ(all_steps[0])
        remaining_yields = {}
        for n, s in enumerate(all_steps):
            I = s["I"]
            lo, hi = blk_range[I]
            addExpLn(s)
            if n + 1 < len(all_steps):
                if n + 1 == hi and I in gens:
                    for _ in gens[I]:
                        pass
                addZ(all_steps[n + 1])
            if n >= 1:
                addExpR(all_steps[n - 1])
            prev = all_steps[n - 1] if (n >= 1 and not s["first"]) else None
            addRupd(s, prev)
            if n >= 1:
                addAV(all_steps[n - 1])
            if I in gens:
                steps_left = hi - n
                if I not in remaining_yields:
                    remaining_yields[I] = 72 if I == order[-1] else (PROJ_OPS_EST + 40 if I == 2 else PROJ_OPS_EST)
                k = max(1, -(-remaining_yields[I] // max(1, steps_left)))
                for _ in range(k):
                    try:
                        next(gens[I])
                        remaining_yields[I] = max(0, remaining_yields[I] - 1)
                    except StopIteration:
                        break
        addExpR(all_steps[-1])
        addAV(all_steps[-1])
        for _ in gens[order[-1]]:
            pass

        barrier_ops = []
        for e in ENGS:
            if P.ops[e]:
                lastop = P.ops[e][-1]
                lastop.signal = True
                barrier_ops.append(lastop)
        barrier_ops += list(P.lastdma.values())
        bar = {}
        bar["pool"] = P.add("pool", lambda g: g.memset(small[:, 8:9], 0.0), deps=barrier_ops)
        bar["dve"] = P.add("dve", lambda v: v.memset(small[:, 9:10], 0.0), deps=barrier_ops)
        bar["act"] = P.add("act", lambda a: a.activation(out=small[:, 10:11], in_=onesf[:, 0:1], func=AF.Copy), deps=barrier_ops)
        bar["pe"] = P.add("pe", lambda t: t.matmul(psum[0:1, 0:1], lhsT=onesb[:, 0:1], rhs=onesb[:, 0:1], start=True, stop=True),
                          deps=barrier_ops)
        barlist = list(bar.values())
        for b_ in (b_small, cbuf):
            pass

        def B2(name=""):
            b = Buf(name)
            b.w = list(barlist)
            return b

        off[0] = PH2_BASE
        wps = sb([4, D], BF16)
        wpc = sb([4, D], BF16)
        wo = sb([8, D], BF16)
        xnT2 = [sb([8, 512], BF16) for _ in range(2)]
        ogs = [sb([4, 512], BF16) for _ in range(2)]
        cqT = sb([4, 512], BF16)
        ckT = sb([4, 1024], BF16)
        cv = sb([8, 512], BF16)
        sZc = sb([4, 512], BF16)
        ogc2 = [sb([4, 512], BF16) for _ in range(2)]
        _o = off[0]
        BTf2 = [sb([640], F32) for _ in range(2)]
        off[0] = _o
        hT = sb([8, 512], BF16)
        Pb = [sb([2, 5, 128], BF16) for _ in range(1)] * 2
        BT = sb([8, 640], BF16)
        sq = [sb([512], BF16) for _ in range(2)]
        sg = [sb([512], F32) for _ in range(2)]
        rst = sg
        t1 = [sb([512], F32)] * 2
        t2 = [sb([512], F32)] * 2
        xres = wst2
        stg2 = [sb([512], F32) for _ in range(3)]
        rden = sb([512], F32)
        rawf = t1[0]
        knw_bc = sb([512], F32)
        small2 = sb([64], F32)

        b_wp = B2("wp")
        b_wo = B2("wo")
        b_xnT2 = [B2() for _ in range(2)]
        b_ogs = [B2() for _ in range(2)]
        b_cqT = [B2() for _ in range(4)]
        b_ckT = [[B2() for _ in range(8)] for _ in range(4)]
        b_cv = [B2() for _ in range(8)]
        b_sZc = [B2() for _ in range(4)]
        b_ogc2 = [B2(), B2()]
        b_hT = B2()
        b_Pb = [B2()] * 2
        b_BT = B2()
        b_BTh = [B2() for _ in range(8)]
        b_sq = [B2() for _ in range(2)]
        b_sg = [B2() for _ in range(2)]
        b_rst = b_sg
        b_t1 = [B2()] * 2
        b_t2 = [B2()] * 2
        b_stg2 = [B2() for _ in range(3)]
        b_xres = b_wst2
        b_rden = B2()
        b_rawf = b_t1[0]
        b_knw = B2()
        b_BTf2 = [b_hT, b_hT]
        b_small2 = B2()
        Sp = psum[:, 0:1280].rearrange("p (h j q) -> p h j q", h=2, j=5)
        EB5 = BT.rearrange("p h (j q) -> p h j q", j=5)
        OB = bank(3).rearrange("p (m q) -> p m q", m=4)
        DB = bank(4).rearrange("p (m q) -> p m q", m=4)
        MS = bank(5)
        b_S = B2()
        b_OB = B2()
        b_DB = B2()
        b_MS = B2()
        b_PJ2 = [B2(), B2()]
        preloaded = {}

        def issue_loads(I):
            xs = nx("x", 2)
            tok0 = blk_tok0(I)
            N = blk_ntok(I)
            P.add("sp", lambda q: q.dma_start(out=xnT2[xs][:, :, 0:N], in_=xnT_d[:, :, tok0:tok0 + N]),
                  reads=[xnTd_buf[I]], writes=[b_xnT2[xs]], key="xn%d" % xs)
            P.add("sp", lambda q: q.dma_start(out=ogs[xs][:, :, 0:N], in_=og_d[:, :, tok0:tok0 + N]),
                  reads=[ogd_buf[I]], writes=[b_ogs[xs]], key="ogl%d" % xs)
            return xs

        preloaded[8] = issue_loads(8)
        for h in range(8):
            P.add("sp", lambda q, h=h: q.dma_start(out=knw_bc[:, h * 64:(h + 1) * 64], in_=bass.AP(knw_d.tensor, 0, [[0, 128], [1, 64]])),
                  writes=[b_knw], key="c1")
        P.add("sp", lambda q: q.dma_start(out=BT, in_=EB_d), reads=[b_EBd], writes=b_BTh, key="ebl")

        CQ, CK, CV, CZ, GS, GC = 0, 512, 1024, 1536, 2048, 3072

        def phase2_block(I, pos, filler):
            slot = I % 2
            ogc = ogc2[pos % 2]
            b_ogc = b_ogc2[pos % 2]
            tok0 = blk_tok0(I)
            N = blk_ntok(I)
            ntile = (N + 127) // 128
            sample = (I == 8)
            want_out = sample or I == 7
            if I in preloaded:
                xs = preloaded[I]
            else:
                xs = issue_loads(I)
            xn = xnT2[xs]
            bxn = b_xnT2[xs]

            def fm(c0, pj):
                for kc in range(8):
                    P.add("pe", lambda t_, kc=kc: t_.matmul(PJ[pj][:, 0:N], lhsT=w2[:, kc, c0:c0 + 128], rhs=xn[:, kc, 0:N],
                                                              start=(kc == 0), stop=(kc == 7)),
                          reads=[b_w2, bxn], writes=[b_PJ2[pj]] if kc == 0 else [])
                b_PJ2[pj].w = [P.ops["pe"][-1]]

            if sample:
                kslot0 = 4
                for kt in range(4):
                    i = nx("wst", 2)
                    P.add("sp", lambda q, i=i, kt=kt: q.dma_start(out=wst2[i][:, 0:512], in_=cck[kt * 128:(kt + 1) * 128, :]),
                          writes=[b_wst2[i]], key="wst%d" % i)
                    sqi = nx("sq", 2)
                    P.add("dve", lambda v, i=i, sqi=sqi: v.tensor_copy(out=sq[sqi], in_=wst2[i][:, 0:512]),
                          reads=[b_wst2[i]], writes=[b_sq[sqi]])
                    pj = nx("pj", 2)
                    ptv = PTb if pj == 0 else PTb1
                    for hp in range(4):
                        P.add("pe", lambda t, hp=hp, sqi=sqi, ptv=ptv: t.transpose(out=ptv[:, hp, :], in_=sq[sqi][:, hp * 128:(hp + 1) * 128],
                                                                                    identity=ident),
                              reads=[b_sq[sqi], cbuf], writes=[b_PJ2[pj]] if hp == 0 else [])
                    b_PJ2[pj].w = [P.ops["pe"][-1]]
                    P.add("dve", lambda v, kt=kt, ptv=ptv: v.tensor_copy(out=ckT[:, :, kt * 128:(kt + 1) * 128], in_=ptv[:, 0:4, :]),
                          reads=[b_PJ2[pj]], writes=[b_ckT[hp][kt] for hp in range(4)])
                    i2 = nx("wst", 2)
                    P.add("sp", lambda q, i2=i2, kt=kt: q.dma_start(out=wst2[i2][:, 0:512], in_=ccv[kt * 128:(kt + 1) * 128, :]),
                          writes=[b_wst2[i2]], key="wst%d" % i2)
                    P.add("pool", lambda g, i2=i2, kt=kt: g.tensor_copy(out=cv[:, kt, :], in_=wst2[i2][:, 0:512]),
                          reads=[b_wst2[i2]], writes=[b_cv[kt]])
                P.add("pool", lambda g: g.memset(ckT[:, :, 512:640], 0.0), writes=[b_ckT[hp][4] for hp in range(4)])
                P.add("pool", lambda g: g.memset(cv[:, 4, :], 0.0), writes=[b_cv[4]])
            else:
                kslot0 = slot * 4

            for which in range(2):
                for hp in range(4):
                    pj = nx("pj", 2)
                    fm((CQ if which == 0 else CK) + hp * 128, pj)
                    sqi = nx("sq", 2)
                    P.add("act", lambda a, pj=pj, sqi=sqi: a.activation(out=sq[sqi][:, 0:N], in_=PJ[pj][:, 0:N], func=AF.Square),
                          reads=[b_PJ2[pj]], writes=[b_sq[sqi]])
                    P.add("pe", lambda t, sqi=sqi: t.matmul(MS[:, 0:N], lhsT=blk64, rhs=sq[sqi][:, 0:N], start=True, stop=True),
                          reads=[b_sq[sqi], cbuf], writes=[b_MS])
                    ri = nx("sg", 2)
                    P.add("act", lambda a, ri=ri: a.activation(out=rst[ri][:, 0:N], in_=MS[:, 0:N], func=AF.Ln, bias=EPS),
                          reads=[b_MS], writes=[b_rst[ri]])
                    P.add("act", lambda a, ri=ri: a.activation(out=rst[ri][:, 0:N], in_=rst[ri][:, 0:N], func=AF.Exp, scale=-0.5),
                          reads=[b_rst[ri]], writes=[b_rst[ri]])
                    if which == 0:
                        P.add("dve", lambda v, pj=pj, ri=ri, hp=hp: v.scalar_tensor_tensor(out=cqT[:, hp, 0:N], in0=PJ[pj][:, 0:N], scalar=qnw8,
                                                                                         in1=rst[ri][:, 0:N], op0=ALU.mult, op1=ALU.mult),
                              reads=[b_PJ2[pj], b_rst[ri], cbuf], writes=[b_cqT[hp]])
                    else:
                        kc0 = kslot0 * 128
                        wl = [b_ckT[hp][kslot0 + j] for j in range(ntile)]
                        P.add("dve", lambda v, pj=pj, ri=ri, hp=hp, kc0=kc0: v.scalar_tensor_tensor(out=ckT[:, hp, kc0:kc0 + N], in0=PJ[pj][:, 0:N],
                                                                                                  scalar=knw1, in1=rst[ri][:, 0:N],
                                                                                                  op0=ALU.mult, op1=ALU.mult),
                              reads=[b_PJ2[pj], b_rst[ri], cbuf], writes=wl)
            for hp in range(4):
                pj = nx("pj", 2)
                fm(CZ + hp * 128, pj)
                gi = nx("sg", 2)
                P.add("act", lambda a, pj=pj, gi=gi: a.activation(out=sg[gi][:, 0:N], in_=PJ[pj][:, 0:N], func=AF.Exp, scale=-1.0),
                      reads=[b_PJ2[pj]], writes=[b_sg[gi]])
                P.add("act", lambda a, gi=gi: a.activation(out=sg[gi][:, 0:N], in_=sg[gi][:, 0:N], func=AF.Ln, bias=1.0),
                      reads=[b_sg[gi]], writes=[b_sg[gi]])
                P.add("act", lambda a, gi=gi: a.activation(out=sg[gi][:, 0:N], in_=sg[gi][:, 0:N], func=AF.Exp, scale=-1.0),
                      reads=[b_sg[gi]], writes=[b_sg[gi]])
                P.add("dve", lambda v, pj=pj, gi=gi, hp=hp: v.tensor_tensor(out=sZc[:, hp, 0:N], in0=PJ[pj][:, 0:N], in1=sg[gi][:, 0:N], op=ALU.mult),
                      reads=[b_PJ2[pj], b_sg[gi]], writes=[b_sZc[hp]])
            for t in range(ntile):
                R = min(128, N - t * 128)
                pj = nx("pj", 2)
                for kc in range(8):
                    P.add("pe", lambda t_, kc=kc, t=t, R=R, pj=pj: t_.matmul(PJ[pj][0:R, :], lhsT=xn[:, kc, t * 128:t * 128 + R],
                                                                            rhs=w2[:, kc, CV:CV + 512], start=(kc == 0), stop=(kc == 7)),
                          reads=[b_w2, bxn], writes=[b_PJ2[pj]] if kc == 0 else [])
                b_PJ2[pj].w = [P.ops["pe"][-1]]
                vt = kslot0 + t
                P.add("dve", lambda v, R=R, vt=vt, pj=pj: v.tensor_copy(out=cv[0:R, vt, :], in_=PJ[pj][0:R, :]),
                      reads=[b_PJ2[pj]], writes=[b_cv[vt]])
                if want_out:
                    si = nx("stg", 3)
                    P.add("dve", lambda v, R=R, si=si, pj=pj: v.tensor_copy(out=stg2[si][0:R, :], in_=PJ[pj][0:R, :]),
                          reads=[b_PJ2[pj]], writes=[b_stg2[si]])
                    dst = cbv_s[0:R, :] if sample else cbv_p[t * 128:t * 128 + R, :]
                    P.add("sp", lambda q, si=si, R=R, dst=dst: q.dma_start(out=dst, in_=stg2[si][0:R, :]), reads=[b_stg2[si]], key="st%d" % si)
                    pj = nx("pj", 2)
                    for kc in range(8):
                        P.add("pe", lambda t_, kc=kc, t=t, R=R, pj=pj: t_.matmul(PJ[pj][0:R, :], lhsT=xn[:, kc, t * 128:t * 128 + R],
                                                                                rhs=w2[:, kc, CK:CK + 512], start=(kc == 0), stop=(kc == 7)),
                              reads=[b_w2, bxn], writes=[b_PJ2[pj]] if kc == 0 else [])
                    b_PJ2[pj].w = [P.ops["pe"][-1]]
                    P.add("dve", lambda v, R=R, pj=pj: v.tensor_copy(out=rawf[0:R, :], in_=PJ[pj][0:R, :]), reads=[b_PJ2[pj]], writes=[b_rawf])
                    si = nx("stg", 3)
                    P.add("dve", lambda v, R=R, si=si: v.tensor_tensor(out=stg2[si][0:R, :], in0=rawf[0:R, :], in1=rawf[0:R, :], op=ALU.mult),
                          reads=[b_rawf], writes=[b_stg2[si]])
                    ssq = small2[0:R, 0:8]
                    P.add("dve", lambda v, R=R, si=si, ssq=ssq: v.tensor_reduce(out=ssq, in_=stg2[si][0:R, :].rearrange("p (h d) -> p h d", h=8),
                                                                                axis=AX.X, op=ALU.add),
                          reads=[b_stg2[si]], writes=[b_small2])
                    P.add("dve", lambda v, ssq=ssq: v.tensor_scalar(ssq, ssq, 1.0 / 64, EPS, ALU.mult, ALU.add), reads=[b_small2], writes=[b_small2])
                    P.add("act", lambda a, ssq=ssq: a.activation(out=ssq, in_=ssq, func=AF.Ln), reads=[b_small2], writes=[b_small2])
                    P.add("act", lambda a, ssq=ssq: a.activation(out=ssq, in_=ssq, func=AF.Exp, scale=-0.5), reads=[b_small2], writes=[b_small2])
                    for h in range(8):
                        P.add("dve", lambda v, h=h, R=R, si=si: v.scalar_tensor_tensor(out=stg2[si][0:R, h * 64:(h + 1) * 64], in0=rawf[0:R, h * 64:(h + 1) * 64],
                                                                                     scalar=small2[0:R, h:h + 1], in1=knw_bc[0:R, h * 64:(h + 1) * 64],
                                                                                     op0=ALU.mult, op1=ALU.mult),
                              reads=[b_rawf, b_small2, b_knw, b_stg2[si]], writes=[b_stg2[si]])
                    dst = cbk_s[0:R, :] if sample else cbk_p[t * 128:t * 128 + R, :]
                    P.add("sp", lambda q, si=si, R=R, dst=dst: q.dma_start(out=dst, in_=stg2[si][0:R, :]), reads=[b_stg2[si]], key="st%d" % si)

            nm = 1 if sample else 4
            for hp in range(4):
                for mm in range(nm):
                    Nq = 64 if sample else 128
                    if sample:
                        jl = list(range(5))
                        kslots = {j: j for j in range(5)}
                    else:
                        mg = 4 * I + mm
                        jl = [j for j in range(5) if mg - 4 + j >= 0]
                        kslots = {j: ((mg - 4 + j) // 4 % 2) * 4 + (mg - 4 + j) % 4 for j in jl}
                    jmin = jl[0]
                    first_s = True
                    njj = 5 - jmin
                    for j in jl:
                        ks = kslots[j]
                        jj = 4 - j
                        for h in range(2):
                            P.add("pe", lambda t, jj=jj, h=h, ks=ks, mm=mm, hp=hp, Nq=Nq: t.matmul(Sp[:, h, jj, 0:Nq], lhsT=ckT[64 * h:64 * h + 64, hp, ks * 128:(ks + 1) * 128],
                                                                                    rhs=cqT[64 * h:64 * h + 64, hp, mm * 128:mm * 128 + Nq], start=True, stop=True),
                                  reads=[b_ckT[hp][ks], b_cqT[hp]], writes=[b_S] if first_s else [])
                            first_s = False
                    b_S.w = [P.ops["pe"][-1]]
                    pi = nx("p", 2)
                    P.add("act", lambda a, pi=pi, njj=njj, Nq=Nq: a.activation(out=Pb[pi][:, :, 0:njj, 0:Nq], in_=Sp[:, :, 0:njj, 0:Nq], func=AF.Exp),
                          reads=[b_S], writes=[b_Pb[pi]])
                    P.add("dve", lambda v, pi=pi, njj=njj, Nq=Nq, hp=hp: v.tensor_tensor(out=Pb[pi][:, :, 0:njj, 0:Nq], in0=Pb[pi][:, :, 0:njj, 0:Nq],
                                                                                   in1=EB5[:, 2 * hp:2 * hp + 2, 0:njj, 0:Nq], op=ALU.mult),
                          reads=[b_Pb[pi], b_BTh[2 * hp], b_BTh[2 * hp + 1]], writes=[b_Pb[pi]])
                    filler(nm * 4 - (hp * nm + mm))
                    first_o = True
                    for j in jl:
                        ks = kslots[j]
                        jj = 4 - j
                        for h in range(2):
                            c0 = hp * 128 + h * 64
                            P.add("pe", lambda t, jj=jj, j=j, h=h, ks=ks, c0=c0, pi=pi, mm=mm, Nq=Nq, jmin=jmin: t.matmul(OB[64 * h:64 * h + 64, mm, 0:Nq], lhsT=cv[:, ks, c0:c0 + 64],
                                                                                            rhs=Pb[pi][:, h, jj, 0:Nq], start=(j == jmin), stop=(j == 4)),
                                  reads=[b_Pb[pi], b_cv[ks]], writes=[b_OB] if (first_o and mm == 0) else [])
                            P.add("pe", lambda t, jj=jj, j=j, h=h, pi=pi, mm=mm, Nq=Nq, jmin=jmin: t.matmul(DB[64 * h:64 * h + 64, mm, 0:Nq], lhsT=onesb[:, 0:64],
                                                                                  rhs=Pb[pi][:, h, jj, 0:Nq], start=(j == jmin), stop=(j == 4)),
                                  reads=[cbuf], writes=[b_DB] if (first_o and mm == 0) else [])
                            first_o = False
                    b_OB.w = [P.ops["pe"][-1]]
                    b_DB.w = [P.ops["pe"][-1]]
                NQ = nm * 128 if not sample else 64
                OBf = bank(3)
                DBf = bank(4)
                P.add("act", lambda a, NQ=NQ: a.activation(out=rden[:, 0:NQ], in_=DBf[:, 0:NQ], func=AF.Ln), reads=[b_DB], writes=[b_rden])
                P.add("act", lambda a, NQ=NQ: a.activation(out=rden[:, 0:NQ], in_=rden[:, 0:NQ], func=AF.Exp, scale=-1.0),
                      reads=[b_rden], writes=[b_rden])
                P.add("dve", lambda v, NQ=NQ: v.tensor_tensor(out=rden[:, 0:NQ], in0=OBf[:, 0:NQ], in1=rden[:, 0:NQ], op=ALU.mult),
                      reads=[b_OB, b_rden], writes=[b_rden])
                P.add("dve", lambda v, NQ=NQ, hp=hp: v.tensor_tensor(out=ogc[:, hp, 0:NQ], in0=rden[:, 0:NQ], in1=sZc[:, hp, 0:NQ], op=ALU.mult),
                      reads=[b_rden, b_sZc[hp]], writes=[b_ogc] if hp == 0 else [])
                if hp != 0:
                    b_ogc.w = [P.ops["dve"][-1]]

            return stageC(I, N, ntile, tok0, xs, xn, bxn, fm, ogc, b_ogc)

        def stageC(I, N, ntile, tok0, xs, xn, bxn, fm, ogc, b_ogc):
            def sigmoid_from(pj, gidx):
                P.add("act", lambda a: a.activation(out=sg[gidx][:, 0:N], in_=PJ[pj][:, 0:N], func=AF.Exp, scale=-1.0),
                      reads=[b_PJ2[pj]], writes=[b_sg[gidx]])
                P.add("act", lambda a: a.activation(out=sg[gidx][:, 0:N], in_=sg[gidx][:, 0:N], func=AF.Ln, bias=1.0),
                      reads=[b_sg[gidx]], writes=[b_sg[gidx]])
                P.add("act", lambda a: a.activation(out=sg[gidx][:, 0:N], in_=sg[gidx][:, 0:N], func=AF.Exp, scale=-1.0),
                      reads=[b_sg[gidx]], writes=[b_sg[gidx]])

            for c in range(8):
                pj = nx("pj", 2)
                fm(GS + c * 128, pj)
                g0 = nx("sg", 2)
                sigmoid_from(pj, g0)
                yield
                pj = nx("pj", 2)
                for hp in range(4):
                    P.add("pe", lambda t, hp=hp, pj=pj, c=c: t.matmul(PJ[pj][:, 0:N], lhsT=wps[:, hp, c * 128:(c + 1) * 128], rhs=ogs[xs][:, hp, 0:N],
                                                                  start=(hp == 0), stop=(hp == 3)),
                          reads=[b_wp, b_ogs[xs]], writes=[b_PJ2[pj]] if hp == 0 else [])
                b_PJ2[pj].w = [P.ops["pe"][-1]]
                ti = nx("t", 2)
                P.add("dve", lambda v, pj=pj, g0=g0, ti=ti: v.tensor_tensor(out=t1[ti][:, 0:N], in0=PJ[pj][:, 0:N], in1=sg[g0][:, 0:N], op=ALU.mult),
                      reads=[b_PJ2[pj], b_sg[g0]], writes=[b_t1[ti]])
                yield
                pj = nx("pj", 2)
                fm(GC + c * 128, pj)
                g1 = nx("sg", 2)
                sigmoid_from(pj, g1)
                yield
                pj = nx("pj", 2)
                for hp in range(4):
                    P.add("pe", lambda t, hp=hp, pj=pj, c=c: t.matmul(PJ[pj][:, 0:N], lhsT=wpc[:, hp, c * 128:(c + 1) * 128], rhs=ogc[:, hp, 0:N],
                                                                  start=(hp == 0), stop=(hp == 3)),
                          reads=[b_wp, b_ogc], writes=[b_PJ2[pj]] if hp == 0 else [])
                b_PJ2[pj].w = [P.ops["pe"][-1]]
                P.add("dve", lambda v, pj=pj, g1=g1, ti=ti: v.tensor_tensor(out=t2[ti][:, 0:N], in0=PJ[pj][:, 0:N], in1=sg[g1][:, 0:N], op=ALU.mult),
                      reads=[b_PJ2[pj], b_sg[g1]], writes=[b_t2[ti]])
                P.add("pool", lambda g, ti=ti, c=c: g.tensor_tensor(out=hT[:, c, 0:N], in0=t1[ti][:, 0:N], in1=t2[ti][:, 0:N], op=ALU.add),
                      reads=[b_t1[ti], b_t2[ti]], writes=[b_hT] if c == 0 else [])
                if c != 0:
                    b_hT.w = [P.ops["pool"][-1]]
                yield
            items = [(t, hf) for t in range(ntile) for hf in range(2)]

            def issue_x(i):
                t, hf = items[i]
                R = min(128, N - t * 128)
                xi = nx("xr", 2)
                P.add("sp", lambda q: q.dma_start(out=xres[xi][0:R, :], in_=x_rows(tok0 + t * 128, R)[:, hf * 512:(hf + 1) * 512]),
                      writes=[b_xres[xi]], key="xr%d" % xi)
                return xi

            def y_cons(i, pj, xi):
                t, hf = items[i]
                R = min(128, N - t * 128)
                si = nx("stg", 3)
                P.add("dve", lambda v: v.tensor_tensor(out=stg2[si][0:R, :], in0=PJ[pj][0:R, :], in1=xres[xi][0:R, :], op=ALU.add),
                      reads=[b_PJ2[pj], b_xres[xi]], writes=[b_stg2[si]])
                P.add("sp", lambda q: q.dma_start(out=y_rows(tok0 + t * 128, R, hf * 512, 512), in_=stg2[si][0:R, :]),
                      reads=[b_stg2[si]], key="st%d" % si)

            xis = {0: issue_x(0)}
            pend = None
            for i, (t, hf) in enumerate(items):
                R = min(128, N - t * 128)
                if pend is not None:
                    y_cons(*pend)
                if i + 1 < len(items):
                    xis[i + 1] = issue_x(i + 1)
                pj = nx("pj", 2)
                for kc in range(8):
                    P.add("pe", lambda t_, kc=kc, t=t, R=R, pj=pj, hf=hf: t_.matmul(PJ[pj][0:R, :], lhsT=hT[:, kc, t * 128:t * 128 + R],
                                                                                   rhs=wo[:, kc, hf * 512:(hf + 1) * 512], start=(kc == 0), stop=(kc == 7)),
                          reads=[b_wo, b_hT], writes=[b_PJ2[pj]] if kc == 0 else [])
                b_PJ2[pj].w = [P.ops["pe"][-1]]
                pend = (i, pj, xis[i])
                yield
            y_cons(*pend)
            yield

        state = {"gen": None, "left": 0}

        def filler(slots_left):
            g = state["gen"]
            if g is None:
                return
            k = max(1, -(-state["left"] // max(1, slots_left)))
            for _ in range(k):
                try:
                    next(g)
                    state["left"] = max(0, state["left"] - 1)
                except StopIteration:
                    state["gen"] = None
                    if state.get("on_done") is not None:
                        state["on_done"]()
                        state["on_done"] = None
                    break

        def drain():
            g = state["gen"]
            if g is not None:
                for _ in g:
                    pass
            state["gen"] = None
            if state.get("on_done") is not None:
                state["on_done"]()
                state["on_done"] = None

        preloaded[0] = issue_loads(0)
        order2 = [8] + list(range(8))
        for pos, I in enumerate(order2):
            gC = phase2_block(I, pos, filler)
            if pos == 0:
                wslots["list"] = [(wst2[0], b_wst2[0], "w2s0"), (wst2[1], b_wst2[1], "w2s1"), (stg2[0], b_stg2[0], "st0"),
                                  (stg2[1], b_stg2[1], "st1"), (stg2[2], b_stg2[2], "st2")]
                for hp in range(4):
                    for hf in range(2):
                        load_w2_chunk(wps_d[hp * 128:(hp + 1) * 128, hf * 512:(hf + 1) * 512], wps[:, hp, hf * 512:(hf + 1) * 512], b_wp)
                        load_w2_chunk(wpc_d[hp * 128:(hp + 1) * 128, hf * 512:(hf + 1) * 512], wpc[:, hp, hf * 512:(hf + 1) * 512], b_wp)
                for kc in range(8):
                    for hf in range(2):
                        load_w2_chunk(wo_d[kc * 128:(kc + 1) * 128, hf * 512:(hf + 1) * 512], wo[:, kc, hf * 512:(hf + 1) * 512], b_wo)
            drain()
            state["gen"] = gC
            state["left"] = 48
            nxt2 = order2[pos + 2] if pos + 2 < len(order2) else None
            if nxt2 is not None and nxt2 not in preloaded:
                state["on_done"] = (lambda nxt2=nxt2: preloaded.__setitem__(nxt2, issue_loads(nxt2)))
        drain()

        print('PH2 end cols', off[0], 'of', ARENA_COLS)
        assert off[0] <= ARENA_COLS
        block = es.enter_context(nc.Block())
        P.emit(block)
    return nc


_CACHE = {}


def kernel(x_prompt, x_sample, cache_sb_k, cache_sb_v, cache_cb_k, cache_cb_v,
           norm_w, w_in, q_norm_w, k_norm_w, rel_bias, w_proj_sb, w_proj_cb, w_out):
    f = lambda a: np.ascontiguousarray(np.asarray(a, dtype=np.float32))
    x_prompt, x_sample = f(x_prompt), f(x_sample)
    cache_sb_k, cache_sb_v, cache_cb_k, cache_cb_v = f(cache_sb_k), f(cache_sb_v), f(cache_cb_k), f(cache_cb_v)
    nw, wi, qn, kn, rb = f(norm_w), f(w_in), f(q_norm_w), f(k_norm_w), f(rel_bias)
    wps, wpc, wo = f(w_proj_sb), f(w_proj_cb), f(w_out)
    n = 8
    if "nc" not in _CACHE:
        _CACHE["nc"] = build_program()
    nc = _CACHE["nc"]
    in_maps = []
    for c in range(n):
        in_maps.append({
            "x_p": x_prompt[c], "x_s": x_sample[c],
            "csk": cache_sb_k[0, c].reshape(PAST, 512), "csv": cache_sb_v[0, c].reshape(PAST, 512),
            "cck": cache_cb_k[0, c].reshape(512, 512), "ccv": cache_cb_v[0, c].reshape(512, 512),
            "norm_w": nw[0:1], "w_in": wi[0], "qnw": qn[0:1], "knw": kn[0:1], "relb": rb[0],
            "wps": wps[0], "wpc": wpc[0], "wo": wo[0],
        })
    res = run_bass_kernel_spmd(nc, in_maps, core_ids=list(range(n)))
    R = res.results
    st = lambda k: np.stack([np.asarray(R[c][k], dtype=np.float32) for c in range(n)])
    y_p = st("y_p")
    y_s = st("y_s")
    sbk_p = st("sbk_p").reshape(1, n, T_P, 8, 64)
    sbv_p = st("sbv_p").reshape(1, n, T_P, 8, 64)
    cbk_p = st("cbk_p").reshape(1, n, 512, 8, 64)
    cbv_p = st("cbv_p").reshape(1, n, 512, 8, 64)
    sbk_s = st("sbk_s").reshape(1, n, T_S, 8, 64)
    sbv_s = st("sbv_s").reshape(1, n, T_S, 8, 64)
    cbk_s = st("cbk_s").reshape(1, n, T_S, 8, 64)
    cbv_s = st("cbv_s").reshape(1, n, T_S, 8, 64)
    if DEBUG:
        _CACHE["dbg"] = R
    return (y_p, y_s, sbk_p, sbv_p, cbk_p, cbv_p, sbk_s, sbv_s, cbk_s, cbv_s)
```

```python
import numpy as np
from contextlib import ExitStack
import concourse.bass as bass
import concourse.mybir as mybir
from concourse.bass_utils import run_bass_kernel_spmd

F32 = mybir.dt.float32
BF16 = mybir.dt.bfloat16
AF = mybir.ActivationFunctionType
ALU = mybir.AluOpType
AX = mybir.AxisListType

T_P = 4096
T_S = 64
TT = T_P + T_S
D = 1024
PAST = 2048
EPS = 1e-6
ENGS = ("pe", "act", "dve", "pool", "sp")
DEBUG = False


class Buf:
    __slots__ = ("w", "r", "name")

    def __init__(self, name=""):
        self.w = []
        self.r = {}
        self.name = name


class Op:
    __slots__ = ("eng", "fn", "deps", "signal", "sem", "val", "isdma", "key")


class Prog:
    def __init__(self, nc, engsem, keysems):
        self.nc = nc
        self.ops = {e: [] for e in ENGS}
        self.engsem = engsem
        self.freekeys = list(keysems)
        self.keysem = {}
        self.keycnt = {}
        self.cnt = {e: 0 for e in ENGS}
        self.lastdma = {}
        self.ndma = 0

    def add(self, eng, fn, reads=(), writes=(), deps=(), key=None):
        op = Op()
        op.eng = eng
        op.fn = fn
        op.signal = False
        op.isdma = key is not None
        op.key = key
        op.sem = None
        op.val = 0
        d = []
        for b in reads:
            d += b.w
        for b in writes:
            d += b.w
            for v in b.r.values():
                d += v
        d += list(deps)
        dd = []
        seen = set()
        for x in d:
            if x is op or id(x) in seen:
                continue
            seen.add(id(x))
            if (not x.isdma) and x.eng == "pe" and eng == "pe":
                continue
            x.signal = True
            dd.append(x)
        op.deps = dd
        for b in reads:
            if op.isdma:
                b.r.setdefault("dma", []).append(op)
            else:
                b.r[eng] = [op]
        for b in writes:
            b.w = [op]
            b.r = {}
        if op.isdma:
            if key not in self.keysem:
                self.keysem[key] = self.freekeys.pop()
                self.keycnt[key] = 0
            self.keycnt[key] += 1
            op.sem = self.keysem[key]
            op.val = 16 * self.keycnt[key]
            self.lastdma[key] = op
            self.ndma += 1
        self.ops[eng].append(op)
        return op

    def emit(self, block):
        for e in ENGS:
            for op in self.ops[e]:
                if (not op.isdma) and op.signal:
                    self.cnt[e] += 1
                    op.sem = self.engsem[e]
                    op.val = self.cnt[e]
        finals = [(self.keysem[k], 16 * self.keycnt[k]) for k in self.keysem]

        def run(e, eng):
            waited = {}
            for op in self.ops[e]:
                need = {}
                for x in op.deps:
                    sid = id(x.sem)
                    if waited.get(sid, 0) >= x.val:
                        continue
                    if sid not in need or need[sid][1] < x.val:
                        need[sid] = (x.sem, x.val)
                need = list(need.values())
                attach = None
                if need and e != "pe":
                    attach = need.pop()
                for (sm, vl) in need:
                    eng.wait_ge(sm, vl)
                    waited[id(sm)] = vl
                ins = op.fn(eng)
                if attach is not None:
                    ins._wait_ge(attach[0], attach[1])
                    waited[id(attach[0])] = attach[1]
                if op.isdma:
                    ins.then_inc(op.sem, 16)
                elif op.signal:
                    ins.then_inc(op.sem, 1)
            if e == "sp":
                for (s, v) in finals:
                    eng.wait_ge(s, v)

        block.tensor(lambda eng: run("pe", eng))
        block.scalar(lambda eng: run("act", eng))
        block.vector(lambda eng: run("dve", eng))
        block.gpsimd(lambda eng: run("pool", eng))
        block.sync(lambda eng: run("sp", eng))


def build_program():
    nc = bass.Bass("TRN2", target_bir_lowering=False)

    def din(name, shape):
        return nc.dram_tensor(name, shape, F32, kind="ExternalInput").ap()

    def dout(name, shape):
        return nc.dram_tensor(name, shape, F32, kind="ExternalOutput").ap()

    x_p = din("x_p", [T_P, D])
    x_s = din("x_s", [T_S, D])
    csk = din("csk", [PAST, 512])
    csv = din("csv", [PAST, 512])
    cck = din("cck", [512, 512])
    ccv = din("ccv", [512, 512])
    norm_w = din("norm_w", [1, D])
    w_in = din("w_in", [D, 6144])
    qnw_d = din("qnw", [1, 64])
    knw_d = din("knw", [1, 64])
    relb = din("relb", [8, 257])
    wps_d = din("wps", [512, D])
    wpc_d = din("wpc", [512, D])
    wo_d = din("wo", [D, D])
    y_p = dout("y_p", [T_P, D])
    y_s = dout("y_s", [T_S, D])
    sbk_p = dout("sbk_p", [T_P, 512])
    sbv_p = dout("sbv_p", [T_P, 512])
    cbk_p = dout("cbk_p", [512, 512])
    cbv_p = dout("cbv_p", [512, 512])
    sbk_s = dout("sbk_s", [T_S, 512])
    sbv_s = dout("sbv_s", [T_S, 512])
    cbk_s = dout("cbk_s", [T_S, 512])
    cbv_s = dout("cbv_s", [T_S, 512])
    skind = dict(kind="ExternalOutput") if DEBUG else {}
    xnT_d = nc.dram_tensor("xnT_d", [128, 8, TT], BF16, **skind).ap()
    og_d = nc.dram_tensor("og_d", [128, 4, TT], BF16, **skind).ap()
    S_d = nc.dram_tensor("S_d", [8, 128, 768], F32).ap()
    EB_d = nc.dram_tensor("EB_d", [128, 8, 640], BF16).ap()

    def x_rows(tok0, R):
        if tok0 >= T_P:
            return x_s[tok0 - T_P:tok0 - T_P + R, :]
        return x_p[tok0:tok0 + R, :]

    def y_rows(tok0, R, c0, cn):
        if tok0 >= T_P:
            return y_s[tok0 - T_P:tok0 - T_P + R, c0:c0 + cn]
        return y_p[tok0:tok0 + R, c0:c0 + cn]

    with ExitStack() as es:
        ARENA_COLS = 53200
        arena = es.enter_context(nc.sbuf_tensor("arena", [128, ARENA_COLS], F32))
        psum = es.enter_context(nc.psum_tensor("psum", [128, 4096], F32))
        engsem = {}
        for e in ("pe", "act", "dve", "pool"):
            engsem[e] = es.enter_context(nc.semaphore("s_" + e))
        keysems = [es.enter_context(nc.semaphore("k%d" % i)) for i in range(80)]
        P = Prog(nc, engsem, keysems)

        off = [0]

        def sb(free_shape, dtype):
            n = 1
            for s in free_shape:
                n *= s
            nb = n * (2 if dtype == BF16 else 4)
            ncol = (nb + 3) // 4
            ap = arena[:, off[0]:off[0] + ncol]
            off[0] += ncol
            assert off[0] <= ARENA_COLS, ("sbuf overflow", off[0])
            if dtype == BF16:
                ap = ap.bitcast(BF16)
            if len(free_shape) == 2:
                ap = ap.rearrange("p (a b) -> p a b", a=free_shape[0])
            elif len(free_shape) == 3:
                ap = ap.rearrange("p (a b c) -> p a b c", a=free_shape[0], b=free_shape[1])
            return ap

        def bank(i, n=1):
            return psum[:, i * 512:(i + n) * 512]

        onesf = sb([128], F32)
        nonesf = sb([128], F32)
        ident = sb([128], BF16)
        nTriI = sb([128], BF16)
        nTriC = sb([128], BF16)
        blk64 = sb([128], BF16)
        onesb = sb([128], BF16)
        normw_bc = sb([D], F32)
        qnw8 = sb([1], F32)
        knw1 = sb([1], F32)
        small = sb([64], F32)
        CONST_END = off[0]
        cbuf = Buf("consts")

        o = P.add("pool", lambda g: g.memset(onesf, 1.0), writes=[cbuf])
        P.add("pool", lambda g: g.memset(nonesf, -1.0), writes=[cbuf])
        P.add("pool", lambda g: g.memset(onesb, 1.0), writes=[cbuf])
        P.add("pool", lambda g: g.affine_select(out=ident, in_=onesf, pattern=[[-1, 128]], compare_op=ALU.is_equal,
                                                fill=0.0, base=0, channel_multiplier=1), reads=[cbuf], writes=[cbuf])
        P.add("pool", lambda g: g.affine_select(out=nTriI, in_=nonesf, pattern=[[-1, 128]], compare_op=ALU.is_ge,
                                                fill=0.0, base=0, channel_multiplier=1), reads=[cbuf], writes=[cbuf])
        P.add("pool", lambda g: g.affine_select(out=nTriC, in_=nonesf, pattern=[[1, 128]], compare_op=ALU.is_gt,
                                                fill=0.0, base=0, channel_multiplier=-1), reads=[cbuf], writes=[cbuf])
        P.add("pool", lambda g: g.memset(blk64, 0.0), writes=[cbuf])
        P.add("pool", lambda g: g.memset(blk64[0:64, 0:64], 1.0 / 64), writes=[cbuf])
        P.add("pool", lambda g: g.memset(blk64[64:128, 64:128], 1.0 / 64), writes=[cbuf])
        cdmas = [P.add("sp", lambda q: q.dma_start(out=normw_bc, in_=bass.AP(norm_w.tensor, 0, [[0, 128], [1, D]])), key="c0")]
        for hh in range(2):
            cdmas.append(P.add("sp", lambda q, hh=hh: q.dma_start(out=qnw8[64 * hh:64 * hh + 64, :],
                                                                  in_=bass.AP(qnw_d.tensor, 0, [[1, 64], [1, 1]])), key="c0"))
            cdmas.append(P.add("sp", lambda q, hh=hh: q.dma_start(out=knw1[64 * hh:64 * hh + 64, :],
                                                                  in_=bass.AP(knw_d.tensor, 0, [[1, 64], [1, 1]])), key="c0"))
        cbuf.w = cbuf.w + cdmas
        P.add("dve", lambda v: v.tensor_scalar(qnw8, qnw8, 0.125, None, ALU.mult), reads=[cbuf], writes=[cbuf])

        NBLK = 9
        xnTd_buf = [Buf("xnTd%d" % i) for i in range(NBLK)]
        ogd_buf = [Buf("ogd%d" % i) for i in range(NBLK)]

        def blk_tok0(I):
            return I * 512

        def blk_ntok(I):
            return 64 if I == 8 else 512

        def slot_of(I):
            return 0 if I == 8 else (I + 1) % 2

        off[0] = CONST_END
        w1 = sb([8, 2048], BF16)
        xt = [sb([D], F32) for _ in range(2)]
        xnb = [sb([D], BF16) for _ in range(2)]
        xnT = [sb([8, 512], BF16) for _ in range(2)]
        stg = [sb([512], F32) for _ in range(4)]
        wst = [sb([1024], F32) for _ in range(2)]
        DEAD_END = off[0]
        KT = sb([4, 4224], BF16)
        Vsb = sb([33, 512], BF16)
        QT = [sb([4, 512], BF16) for _ in range(2)]
        sZ = [sb([4, 512], BF16) for _ in range(2)]
        og = [sb([4, 512], BF16) for _ in range(2)]
        ebuf = [sb([2, 512], F32) for _ in range(3)]
        spb = [sb([2, 512], BF16) for _ in range(3)]
        Peb = [sb([2, 512], F32) for _ in range(2)]
        Wb = [sb([2, 512], BF16) for _ in range(2)]
        PH1_END = off[0]

        b_w1 = Buf("w1")
        b_KT = [[Buf("KT%d_%d" % (hp, j)) for j in range(33)] for hp in range(4)]
        b_V = [Buf("V%d" % j) for j in range(33)]
        b_xt = [Buf() for _ in range(2)]
        b_xnb = [Buf() for _ in range(2)]
        b_xnT = [Buf() for _ in range(2)]
        b_QT = [[Buf() for _ in range(4)] for _ in range(2)]
        b_sZ = [[Buf() for _ in range(4)] for _ in range(2)]
        b_og = [Buf() for _ in range(2)]
        b_e = [Buf() for _ in range(3)]
        b_Pe = [Buf() for _ in range(2)]
        b_sp = [Buf() for _ in range(3)]
        b_W = [Buf() for _ in range(2)]
        b_stg = [Buf() for _ in range(4)]
        b_wst = [Buf() for _ in range(2)]
        b_small = Buf()
        Zp = psum[:, 0:1024].rearrange("p (h n) -> p h n", h=2)
        Rp = psum[:, 1024:2048].rearrange("p (h n) -> p h n", h=2)
        Op_ = [bank(4), bank(5)]
        PJ = [bank(6), bank(7)]
        b_Z = Buf("Z")
        b_R = Buf("R")
        b_O = [Buf("O0"), Buf("O1")]
        b_PJ = [Buf("PJ0"), Buf("PJ1")]
        cnt = {"pj": 0, "stg": 0, "wst": 0, "xt": 0}

        def next_pj():
            i = cnt["pj"] % 2
            cnt["pj"] += 1
            return i

        def next_stg():
            i = cnt["stg"] % 4
            cnt["stg"] += 1
            return i

        stage_slots = [(wst[0], b_wst[0], "wst0"), (wst[1], b_wst[1], "wst1"), (xt[0], b_xt[0], "xt0"), (xt[1], b_xt[1], "xt1")]

        def next_stage():
            i = cnt["wst"] % 4
            cnt["wst"] += 1
            return stage_slots[i]

        def load_w_chunk(src_ap, dst_ap, dst_buf):
            st_ap, st_buf, st_key = [stage_slots[0], stage_slots[1], stage_slots[3]][cnt["wst"] % 3]
            cnt["wst"] += 1
            ncol = src_ap.shape[1]
            eng = "dve"
            P.add("sp", lambda q: q.dma_start(out=st_ap[:, 0:ncol], in_=src_ap), writes=[st_buf], key=st_key)
            P.add(eng, lambda v: v.tensor_copy(out=dst_ap, in_=st_ap[:, 0:ncol]), reads=[st_buf], writes=[dst_buf])

        b_w1c = [Buf("w1c%d" % g_) for g_ in range(4)]

        def load_w1():
            for g_ in range(4):
                for kc in range(8):
                    load_w_chunk(w_in[kc * 128:(kc + 1) * 128, g_ * 512:(g_ + 1) * 512],
                                 w1[:, kc, g_ * 512:(g_ + 1) * 512], b_w1c[g_])

        P.add("pool", lambda g: g.memset(KT[:, :, 4096:4224], 0.0), writes=[b_KT[hp][32] for hp in range(4)])
        P.add("pool", lambda g: g.memset(Vsb[:, 32, :], 0.0), writes=[b_V[32]])
        PTb = PJ[0].bitcast(BF16).rearrange("p (a b) -> p a b", a=8)
        PTb1 = PJ[1].bitcast(BF16).rearrange("p (a b) -> p a b", a=8)
        b_Sd = Buf("Sd")

        def startup_late():
            for kt in range(15, -1, -1):
                st_ap, st_buf, st_key = next_stage()
                P.add("sp", lambda q, st_ap=st_ap, kt=kt: q.dma_start(out=st_ap[:, 0:512], in_=csk[kt * 128:(kt + 1) * 128, :]),
                      writes=[st_buf], key=st_key)
                xs = cnt["xt"] % 2
                cnt["xt"] += 1
                P.add("dve", lambda v, st_ap=st_ap, xs=xs: v.tensor_copy(out=xnb[xs][:, 0:512], in_=st_ap[:, 0:512]),
                      reads=[st_buf], writes=[b_xnb[xs]])
                pj = next_pj()
                ptv = PTb if pj == 0 else PTb1
                for hp in range(4):
                    P.add("pe", lambda t, hp=hp, xs=xs, ptv=ptv: t.transpose(out=ptv[:, hp, :], in_=xnb[xs][:, hp * 128:(hp + 1) * 128],
                                                                              identity=ident),
                          reads=[b_xnb[xs], cbuf], writes=[b_PJ[pj]] if hp == 0 else [], deps=[])
                lastT = P.ops["pe"][-1]
                b_PJ[pj].w = [lastT]
                P.add("dve", lambda v, kt=kt, ptv=ptv: v.tensor_copy(out=KT[:, :, (16 + kt) * 128:(17 + kt) * 128], in_=ptv[:, 0:4, :]),
                      reads=[b_PJ[pj]], writes=[b_KT[hp][16 + kt] for hp in range(4)])
                st2_ap, st2_buf, st2_key = next_stage()
                P.add("sp", lambda q, st2_ap=st2_ap, kt=kt: q.dma_start(out=st2_ap[:, 0:512], in_=csv[kt * 128:(kt + 1) * 128, :]),
                      writes=[st2_buf], key=st2_key)
                P.add("act", lambda a, st2_ap=st2_ap, kt=kt: a.activation(out=Vsb[:, 16 + kt, :], in_=st2_ap[:, 0:512], func=AF.Copy),
                      reads=[st2_buf], writes=[b_V[16 + kt]])

        RB1 = sb([768], F32)
        b_RB1 = Buf("RB1")
        def rb_chain_gen():
            for h in range(8):
                P.add("sp", lambda q, h=h: q.dma_start(out=RB1[:, 0:257], in_=bass.AP(relb.tensor, h * 257, [[0, 128], [1, 257]])),
                      writes=[b_RB1], key="rb")
                yield
                yield
                yield
                P.add("pool", lambda g: g.memset(RB1[:, 257:768], 0.0), reads=[b_RB1], writes=[b_RB1])
                P.add("dve", lambda v: v.tensor_scalar(RB1[:, 257:768], RB1[:, 257:768], RB1[:, 256:257], None, ALU.add),
                      reads=[b_RB1], writes=[b_RB1])
                yield
                P.add("sp", lambda q, h=h: q.dma_start(out=S_d[h], in_=RB1), reads=[b_RB1], writes=[b_Sd] if h == 0 else [], key="rb2")
                if h != 0:
                    b_Sd.w = b_Sd.w + [P.ops["sp"][-1]]


                yield

        b_smallt = [Buf() for _ in range(4)]

        prefetched = {}

        def proj1(I, nxt=None):
            slot = slot_of(I)
            tok0 = blk_tok0(I)
            ntok = blk_ntok(I)
            ntile = (ntok + 127) // 128
            sample = (I == 8)
            tinfo = []
            for t in range(ntile):
                R = min(128, ntok - t * 128)
                tinfo.append(dict(t=t, R=R, r0=tok0 + t * 128, ss=small[0:R, 4 * t:4 * t + 1], ms=small[0:R, 4 * t + 1:4 * t + 2],
                                  ln=small[0:R, 4 * t + 2:4 * t + 3], rs=small[0:R, 4 * t + 3:4 * t + 4], bs=b_smallt[t]))

            def stA(ti):
                ti["xs"] = cnt["xt"] % 2
                cnt["xt"] += 1
                xs, r0, R = ti["xs"], ti["r0"], ti["R"]
                P.add("sp", lambda q: q.dma_start(out=xt[xs][0:R, :], in_=x_rows(r0, R)), writes=[b_xt[xs]], key="xt%d" % xs)

            def stB(ti):
                xs, R, ss, ms, bs = ti["xs"], ti["R"], ti["ss"], ti["ms"], ti["bs"]
                P.add("dve", lambda v: v.scalar_tensor_tensor(out=xnb[xs][0:R, :], in0=xt[xs][0:R, :], scalar=1.0, in1=xt[xs][0:R, :],
                                                              op0=ALU.mult, op1=ALU.mult, accum_out=ss),
                      reads=[b_xt[xs]], writes=[b_xnb[xs], bs])
                P.add("dve", lambda v: v.tensor_scalar(ms, ss, 1.0 / D, EPS, ALU.mult, ALU.add), reads=[bs], writes=[bs])

            def stC(ti):
                ms, ln, rs, bs = ti["ms"], ti["ln"], ti["rs"], ti["bs"]
                P.add("act", lambda a: a.activation(out=ln, in_=ms, func=AF.Ln), reads=[bs], writes=[bs])
                P.add("act", lambda a: a.activation(out=rs, in_=ln, func=AF.Exp, scale=-0.5), reads=[bs], writes=[bs])

            def stD(ti):
                xs, R, rs, bs = ti["xs"], ti["R"], ti["rs"], ti["bs"]
                P.add("dve", lambda v: v.scalar_tensor_tensor(out=xnb[xs][0:R, :], in0=xt[xs][0:R, :], scalar=rs, in1=normw_bc[0:R, :],
                                                              op0=ALU.mult, op1=ALU.mult),
                      reads=[b_xt[xs], bs, cbuf], writes=[b_xnb[xs]])

            def stE(ti):
                xs, R = ti["xs"], ti["R"]
                pj = next_pj()
                ti["pj"] = pj
                ptv = PTb if pj == 0 else PTb1
                for kc in range(8):
                    P.add("pe", lambda t_, kc=kc: t_.transpose(out=ptv[:, kc, 0:R], in_=xnb[xs][0:R, kc * 128:(kc + 1) * 128],
                                                                identity=ident[0:R, 0:R]),
                          reads=[b_xnb[xs], cbuf], writes=[b_PJ[pj]] if kc == 0 else [])
                b_PJ[pj].w = [P.ops["pe"][-1]]

            def stF(ti):
                t, R, pj = ti["t"], ti["R"], ti["pj"]
                ptv = PTb if pj == 0 else PTb1
                P.add("dve", lambda v: v.tensor_copy(out=xnT[slot][:, :, t * 128:t * 128 + R], in_=ptv[:, :, 0:R]),
                      reads=[b_PJ[pj]], writes=[b_xnT[slot]])

            pre = prefetched.get(I, None)
            if pre is not None:
                for t in range(min(2, ntile)):
                    tinfo[t]["xs"] = pre[t]
            base = 0 if pre is not None else 3
            sched = {}

            def at(k, fn, ti):
                sched.setdefault(k, []).append((fn, ti))

            for t in range(ntile):
                ti = tinfo[t]
                if t < 2:
                    if pre is None:
                        at(0, stA, ti)
                    b0 = base + 2 * t
                else:
                    b0 = base + 2 * (t - 2) + 16
                for off_, fn in zip([0, 3, 5, 6, 8], [stB, stC, stD, stE, stF]):
                    at(b0 + off_, fn, ti)
                if t + 2 < ntile:
                    at(b0 + 5, stA, tinfo[t + 2])
            for k in range(max(sched) + 1):
                for (fn, ti) in sched.get(k, []):
                    fn(ti)
                yield
            P.add("sp", lambda q: q.dma_start(out=xnT_d[:, :, tok0:tok0 + ntok], in_=xnT[slot][:, :, 0:ntok]),
                  reads=[b_xnT[slot]], writes=[xnTd_buf[I]], key="xs%d" % slot)
            yield

            def fm_mm(c0, pj):
                for kc in range(8):
                    P.add("pe", lambda t_, kc=kc: t_.matmul(PJ[pj][:, 0:ntok], lhsT=w1[:, kc, c0:c0 + 128], rhs=xnT[slot][:, kc, 0:ntok],
                                                              start=(kc == 0), stop=(kc == 7)),
                          reads=[b_w1c[c0 // 512], b_xnT[slot]], writes=[b_PJ[pj]] if kc == 0 else [])
                    b_PJ[pj].w = [P.ops["pe"][-1]]
                    if kc % 2 == 1:
                        yield

            def tm_mm(t, R, c0, pj):
                for kc in range(8):
                    P.add("pe", lambda t_, kc=kc: t_.matmul(PJ[pj][0:R, :], lhsT=xnT[slot][:, kc, t * 128:t * 128 + R],
                                                              rhs=w1[:, kc, c0:c0 + 512], start=(kc == 0), stop=(kc == 7)),
                          reads=[b_w1c[c0 // 512], b_xnT[slot]], writes=[b_PJ[pj]] if kc == 0 else [])
                    b_PJ[pj].w = [P.ops["pe"][-1]]
                    if kc % 2 == 1:
                        yield

            def cons_q(hp, pj):
                P.add("dve", lambda v: v.tensor_scalar(QT[slot][:, hp, 0:ntok], PJ[pj][:, 0:ntok], 0.125, None, ALU.mult),
                      reads=[b_PJ[pj]], writes=[b_QT[slot][hp]])

            def cons_k(hp, pj):
                kcol = 4096 if sample else tok0
                kb_list = [b_KT[hp][32]] if sample else [b_KT[hp][tok0 // 128 + j] for j in range(4)]
                P.add("dve", lambda v: v.tensor_copy(out=KT[:, hp, kcol:kcol + ntok], in_=PJ[pj][:, 0:ntok]),
                      reads=[b_PJ[pj]], writes=kb_list)

            def cons_z(hp, pj):
                si = next_stg()
                P.add("act", lambda a: a.activation(out=stg[si][:, 0:ntok], in_=PJ[pj][:, 0:ntok], func=AF.Exp, scale=-1.0),
                      reads=[b_PJ[pj]], writes=[b_stg[si]])
                P.add("act", lambda a: a.activation(out=stg[si][:, 0:ntok], in_=stg[si][:, 0:ntok], func=AF.Ln, bias=1.0),
                      reads=[b_stg[si]], writes=[b_stg[si]])
                P.add("act", lambda a: a.activation(out=stg[si][:, 0:ntok], in_=stg[si][:, 0:ntok], func=AF.Exp, scale=-1.0),
                      reads=[b_stg[si]], writes=[b_stg[si]])
                P.add("dve", lambda v: v.tensor_tensor(out=sZ[slot][:, hp, 0:ntok], in0=PJ[pj][:, 0:ntok], in1=stg[si][:, 0:ntok], op=ALU.mult),
                      reads=[b_stg[si], b_PJ[pj]], writes=[b_sZ[slot][hp]])

            def cons_tm(t, R, which, pj):
                si = next_stg()
                P.add("dve", lambda v: v.tensor_copy(out=stg[si][0:R, :], in_=PJ[pj][0:R, :]), reads=[b_PJ[pj]], writes=[b_stg[si]])
                if sample:
                    dst = (sbk_s if which == 0 else sbv_s)[0:R, :]
                else:
                    dst = (sbk_p if which == 0 else sbv_p)[tok0 + t * 128:tok0 + t * 128 + R, :]
                if which == 1:
                    vt = 32 if sample else (tok0 // 128 + t)
                    P.add("pool", lambda g: g.tensor_copy(out=Vsb[0:R, vt, :], in_=stg[si][0:R, :]), reads=[b_stg[si]], writes=[b_V[vt]])
                P.add("sp", lambda q: q.dma_start(out=dst, in_=stg[si][0:R, :]), reads=[b_stg[si]], key="st%d" % si)

            groups = []
            for hp in range(4):
                groups.append((lambda pj, hp=hp: fm_mm(hp * 128, pj), lambda pj, hp=hp: cons_q(hp, pj)))
                groups.append((lambda pj, hp=hp: fm_mm(512 + hp * 128, pj), lambda pj, hp=hp: cons_k(hp, pj)))
                groups.append((lambda pj, hp=hp: fm_mm(1536 + hp * 128, pj), lambda pj, hp=hp: cons_z(hp, pj)))
            for t in range(ntile):
                R = min(128, ntok - t * 128)
                for which in range(2):
                    c0 = 512 if which == 0 else 1024
                    groups.append((lambda pj, t=t, R=R, c0=c0: tm_mm(t, R, c0, pj), lambda pj, t=t, R=R, which=which: cons_tm(t, R, which, pj)))
            pending = None
            for (mmf, consf) in groups:
                pj = next_pj()
                yield from mmf(pj)
                if pending is not None:
                    pending()
                    yield
                pending = (lambda consf=consf, pj=pj: consf(pj))
            pending()
            yield
            if nxt is not None:
                nt_n = (blk_ntok(nxt) + 127) // 128
                lst = []
                for t in range(min(2, nt_n)):
                    R = min(128, blk_ntok(nxt) - t * 128)
                    ti = dict(R=R, r0=blk_tok0(nxt) + t * 128)
                    stA(ti)
                    lst.append(ti["xs"])
                prefetched[nxt] = lst
                yield

        def make_steps(I):
            slot = slot_of(I)
            N = blk_ntok(I)
            steps = []
            if I == 8:
                for idx, J in enumerate(range(16, -1, -1)):
                    st = dict(I=I, hp=0, J=16 + J, slot=slot, N=256, first=(idx == 0), last=(idx == 16), all=True,
                              diag=(0 if J == 16 else None), c0=0)
                    steps.append(st)
                return steps
            for hp in range(4):
                if I == 8:
                    Js = list(range(16, -1, -1))
                else:
                    Js = list(range(4 * I + 3, -1, -1))
                for idx, J in enumerate(Js):
                    st = dict(I=I, hp=hp, J=(16 + J if I == 8 else J), slot=slot, N=N, first=(idx == 0), last=(idx == len(Js) - 1))
                    if I == 8:
                        st["diag"] = 0 if J == 16 else None
                    else:
                        st["diag"] = (J - 4 * I) * 128 if J >= 4 * I else None
                    st["c0"] = st["diag"] if st["diag"] is not None else 0
                    steps.append(st)
            return steps

        gcount = {"n": 0, "o": 0}

        def addZ(s):
            hp, J, N, slot, c0 = s["hp"], s["J"], s["N"], s["slot"], s["c0"]
            if s.get("all"):
                first = True
                for hp_ in range(4):
                    for h in range(2):
                        P.add("pe", lambda t, h=h, hp_=hp_: t.matmul(Zp[:, h, hp_ * 64:(hp_ + 1) * 64], lhsT=KT[64 * h:64 * h + 64, hp_, J * 128:(J + 1) * 128],
                                                                     rhs=QT[slot][64 * h:64 * h + 64, hp_, 0:64], start=True, stop=True),
                              reads=[b_KT[q_][J] for q_ in range(4)] + [b_QT[slot][q_] for q_ in range(4)] if first else [],
                              writes=[b_Z] if first else [])
                        first = False
                b_Z.w = [P.ops["pe"][-1]]
                return
            for h in range(2):
                P.add("pe", lambda t, h=h: t.matmul(Zp[:, h, c0:N], lhsT=KT[64 * h:64 * h + 64, hp, J * 128:(J + 1) * 128],
                                                     rhs=QT[slot][64 * h:64 * h + 64, hp, c0:N], start=True, stop=True),
                      reads=[b_KT[hp][J], b_QT[slot][hp]], writes=[b_Z] if h == 0 else [])
            b_Z.w = [P.ops["pe"][-1]]

        def addExpLn(s):
            n = s["n"]
            N, c0 = s["N"], s["c0"]
            e = ebuf[n % 3]
            spx = spb[n % 3]
            b_e[n % 3].r = {}
            P.add("act", lambda a: a.activation(out=e[:, :, c0:N], in_=Zp[:, :, c0:N], func=AF.Exp), reads=[b_Z], writes=[b_e[n % 3]])
            P.add("act", lambda a: a.activation(out=spx[:, :, c0:N], in_=e[:, :, c0:N], func=AF.Ln, bias=1.0),
                  reads=[b_e[n % 3]], writes=[b_sp[n % 3]])
            if s["diag"] is not None:
                base = c0 - s["diag"]
                pat = [[0, 2], [0, 4], [1, 64]] if s.get("all") else [[0, 2], [1, N - c0]]
                P.add("pool", lambda g: g.affine_select(out=spx[:, :, c0:N], in_=spx[:, :, c0:N], pattern=pat,
                                                        compare_op=ALU.is_gt, fill=0.0, base=base, channel_multiplier=-1),
                      reads=[b_sp[n % 3]], writes=[b_sp[n % 3]])

        def addRupd(s, prev):
            n, N, c0 = s["n"], s["N"], s["c0"]
            spx = spb[n % 3]
            first = s["first"]
            ops = []
            if not first:
                spp = spb[prev["n"] % 3]
                pc0 = prev["c0"]
                for h in range(2):
                    ops.append(lambda t, h=h: t.matmul(Rp[:, h, pc0:N], lhsT=nTriC, rhs=spp[:, h, pc0:N], start=False, stop=True,
                                                       skip_group_check=True))
            for h in range(2):
                ops.append(lambda t, h=h: t.matmul(Rp[:, h, c0:N], lhsT=nTriI, rhs=spx[:, h, c0:N], start=first, stop=True,
                                                   skip_group_check=True))
            rd = [b_sp[n % 3], cbuf]
            if not first:
                rd += [b_sp[prev["n"] % 3]]
            for i, fn in enumerate(ops):
                P.add("pe", fn, reads=rd if i == 0 else [], writes=[b_R] if i == 0 else [])
            b_R.w = [P.ops["pe"][-1]]

        def addExpR(s):
            n, hp, J, N, slot, I, c0 = s["n"], s["hp"], s["J"], s["N"], s["slot"], s["I"], s["c0"]
            W = Wb[n % 2]
            Pe = Peb[n % 2]
            e = ebuf[n % 3]
            b_Pe[n % 2].r = {}
            P.add("act", lambda a: a.activation(out=Pe[:, :, c0:N], in_=Rp[:, :, c0:N], func=AF.Exp), reads=[b_R], writes=[b_Pe[n % 2]])
            P.add("dve", lambda v: v.tensor_tensor(out=W[:, :, c0:N], in0=Pe[:, :, c0:N], in1=e[:, :, c0:N], op=ALU.mult),
                  reads=[b_Pe[n % 2], b_e[n % 3]], writes=[b_W[n % 2]])
            if s["diag"] is not None:
                base = c0 - s["diag"]
                pat = [[0, 2], [0, 4], [1, 64]] if s.get("all") else [[0, 2], [1, N - c0]]
                P.add("pool", lambda g: g.affine_select(out=W[:, :, c0:N], in_=W[:, :, c0:N], pattern=pat,
                                                        compare_op=ALU.is_gt, fill=0.0, base=base, channel_multiplier=-1),
                      reads=[b_W[n % 2]], writes=[b_W[n % 2]])

        def addAV(s):
            n, hp, J, N, slot, I, c0 = s["n"], s["hp"], s["J"], s["N"], s["slot"], s["I"], s["c0"]
            W = Wb[n % 2]
            if s["first"]:
                gcount["o"] += 1
            oi = gcount["o"] % 2
            s["oi"] = oi
            if s.get("all"):
                first = True
                for hp_ in range(4):
                    for h in range(2):
                        cc = hp_ * 128 + h * 64
                        P.add("pe", lambda t, h=h, hp_=hp_, cc=cc: t.matmul(Op_[oi][64 * h:64 * h + 64, hp_ * 64:(hp_ + 1) * 64], lhsT=Vsb[:, J, cc:cc + 64],
                                                                            rhs=W[:, h, hp_ * 64:(hp_ + 1) * 64], start=(s["first"] and hp_ == 0),
                                                                            stop=s["last"], skip_group_check=True),
                              reads=[b_W[n % 2], b_V[J]] if first else [], writes=[b_O[oi]] if (first and s["first"]) else [])
                        first = False
                b_O[oi].w = [P.ops["pe"][-1]]
                if s["last"]:
                    P.add("dve", lambda v: v.tensor_tensor(out=og[slot][:, :, 0:64], in0=Op_[oi][:, 0:256].rearrange("p (a b) -> p a b", a=4),
                                                           in1=sZ[slot][:, :, 0:64], op=ALU.mult),
                          reads=[b_O[oi]] + [b_sZ[slot][q_] for q_ in range(4)], writes=[b_og[slot]])
                    tok0 = blk_tok0(I)
                    P.add("sp", lambda q: q.dma_start(out=og_d[:, :, tok0:tok0 + 64], in_=og[slot][:, :, 0:64]),
                          reads=[b_og[slot]], writes=[ogd_buf[I]], key="og%d" % slot)
                return
            for h in range(2):
                cc = hp * 128 + h * 64
                P.add("pe", lambda t, h=h, cc=cc: t.matmul(Op_[oi][64 * h:64 * h + 64, c0:N], lhsT=Vsb[:, J, cc:cc + 64], rhs=W[:, h, c0:N],
                                                            start=s["first"], stop=s["last"], skip_group_check=True),
                      reads=[b_W[n % 2], b_V[J]] if h == 0 else [], writes=[b_O[oi]] if (h == 0 and s["first"]) else [])
            b_O[oi].w = [P.ops["pe"][-1]]
            if s["last"]:
                P.add("dve", lambda v: v.tensor_tensor(out=og[slot][:, hp, 0:N], in0=Op_[oi][:, 0:N], in1=sZ[slot][:, hp, 0:N], op=ALU.mult),
                      reads=[b_O[oi], b_sZ[slot][hp]], writes=[b_og[slot]] if hp == 0 else [])
                if hp != 0:
                    b_og[slot].w = [P.ops["dve"][-1]]
                if hp == 3:
                    tok0 = blk_tok0(I)
                    P.add("sp", lambda q: q.dma_start(out=og_d[:, :, tok0:tok0 + N], in_=og[slot][:, :, 0:N]),
                          reads=[b_og[slot]], writes=[ogd_buf[I]], key="og%d" % slot)

        print('CONST_END', CONST_END, 'DEAD_END', DEAD_END, 'PH1 end cols', off[0])
        _save = off[0]
        off[0] = CONST_END
        w2 = sb([8, 4096], BF16)
        wst2 = [sb([512], F32) for _ in range(2)]
        BTf_e = [sb([640], F32) for _ in range(2)]
        EBs_e = [sb([640], BF16) for _ in range(2)]
        assert off[0] <= DEAD_END, (off[0], DEAD_END)
        PH2_BASE = CONST_END + 8 * 4096 // 2 + 2 * 512
        b_BTe = [Buf(), Buf()]
        b_EBs = [Buf(), Buf()]
        b_EBd = Buf("EBd")
        off[0] = _save
        b_w2 = Buf("w2")
        b_wst2 = [Buf(), Buf()]
        c2 = {"pj": 0, "stg": 0, "wst": 0, "x": 0, "xr": 0, "wsl": 0, "p": 0, "sq": 0, "sg": 0, "t": 0}

        def nx(k, m):
            i = c2[k] % m
            c2[k] += 1
            return i

        wslots = {"list": None}

        def load_w2_chunk(src_ap, dst_ap, dst_buf):
            ncol = src_ap.shape[1]
            if wslots["list"] is None:
                i = nx("wst", 2)
                st_ap, st_buf, st_key = wst2[i], b_wst2[i], "w2s%d" % i
            else:
                st_ap, st_buf, st_key = wslots["list"][nx("wsl", len(wslots["list"]))]
            P.add("sp", lambda q: q.dma_start(out=st_ap[:, 0:ncol], in_=src_ap), writes=[st_buf], key=st_key)
            ceng = "dve" if wslots["list"] is None else "pool"
            P.add(ceng, lambda v: v.tensor_copy(out=dst_ap, in_=st_ap[:, 0:ncol]), reads=[st_buf], writes=[dst_buf])

        def w2_loader():
            early = [P.ops[e][-1] for e in ("pe", "act", "dve", "pool") if P.ops[e]] + list(P.lastdma.values())
            b_w2.w = list(early)
            b_wst2[0].w = list(early)
            b_wst2[1].w = list(early)
            for kc in range(8):
                for q8 in range(8):
                    load_w2_chunk(w_in[kc * 128:(kc + 1) * 128, 2048 + q8 * 512:2048 + (q8 + 1) * 512],
                                  w2[:, kc, q8 * 512:(q8 + 1) * 512], b_w2)
                    yield
            for i_ in range(2):
                b_BTe[i_].w = list(early)
                b_EBs[i_].w = list(early)
            for h in range(8):
                bt, bb, eb, be = BTf_e[h % 2], b_BTe[h % 2], EBs_e[h % 2], b_EBs[h % 2]
                P.add("sp", lambda q, h=h, bt=bt: q.dma_start(out=bt, in_=bass.AP(S_d.tensor, h * 128 * 768 + 128, [[767, 128], [1, 640]])),
                      reads=[b_Sd], writes=[bb], key="rb3%d" % (h % 2))
                P.add("pool", lambda g, bt=bt: g.memset(bt[0:64, 576:640], -30000.0), reads=[bb], writes=[bb])
                P.add("pool", lambda g, bt=bt: g.memset(bt[64:128, 0:64], -30000.0), reads=[bb], writes=[bb])
                P.add("act", lambda a_, bt=bt, eb=eb: a_.activation(out=eb, in_=bt, func=AF.Exp), reads=[bb], writes=[be])
                P.add("sp", lambda q, h=h, eb=eb: q.dma_start(out=EB_d[:, h, :], in_=eb), reads=[be], writes=[b_EBd] if h == 0 else [],
                      key="ebd%d" % (h % 2))
                if h != 0:
                    b_EBd.w = b_EBd.w + [P.ops["sp"][-1]]
                yield

        order = [8] + list(range(8))
        g_first = proj1(order[0], None)
        for _ in range(13):
            next(g_first)
        load_w1()
        for _ in g_first:
            pass
        startup_late()
        all_steps = []
        blk_range = {}
        for I in order:
            st = make_steps(I)
            blk_range[I] = (len(all_steps), len(all_steps) + len(st))
            all_steps += st
        for n, s in enumerate(all_steps):
            s["n"] = n
        gens = {}
        for bi, I in enumerate(order):
            if bi + 1 < len(order):
                gens[I] = proj1(order[bi + 1], order[bi + 2] if bi + 2 < len(order) else None)
        gens[order[-1]] = w2_loader()

        def chain_gens(*gs):
            for g_ in gs:
                yield from g_

        gens[2] = chain_gens(gens[2], rb_chain_gen())
        PROJ_OPS_EST = 140
        addZ(all_steps[0])
        remaining_yields = {}
        for n, s in enumerate(all_steps):
            I = s["I"]
            lo, hi = blk_range[I]
            addExpLn(s)
            if n + 1 < len(all_steps):
                if n + 1 == hi and I in gens:
                    for _ in gens[I]:
                        pass
                addZ(all_steps[n + 1])
            if n >= 1:
                addExpR(all_steps[n - 1])
            prev = all_steps[n - 1] if (n >= 1 and not s["first"]) else None
            addRupd(s, prev)
            if n >= 1:
                addAV(all_steps[n - 1])
            if I in gens:
                steps_left = hi - n
                if I not in remaining_yields:
                    remaining_yields[I] = 72 if I == order[-1] else (PROJ_OPS_EST + 40 if I == 2 else PROJ_OPS_EST)
                k = max(1, -(-remaining_yields[I] // max(1, steps_left)))
                for _ in range(k):
                    try:
                        next(gens[I])
                        remaining_yields[I] = max(0, remaining_yields[I] - 1)
                    except StopIteration:
                        break
        addExpR(all_steps[-1])
        addAV(all_steps[-1])
        for _ in gens[order[-1]]:
            pass

        barrier_ops = []
        for e in ENGS:
            if P.ops[e]:
                lastop = P.ops[e][-1]
                lastop.signal = True
                barrier_ops.append(lastop)
        barrier_ops += list(P.lastdma.values())
        bar = {}
        bar["pool"] = P.add("pool", lambda g: g.memset(small[:, 8:9], 0.0), deps=barrier_ops)
        bar["dve"] = P.add("dve", lambda v: v.memset(small[:, 9:10], 0.0), deps=barrier_ops)
        bar["act"] = P.add("act", lambda a: a.activation(out=small[:, 10:11], in_=onesf[:, 0:1], func=AF.Copy), deps=barrier_ops)
        bar["pe"] = P.add("pe", lambda t: t.matmul(psum[0:1, 0:1], lhsT=onesb[:, 0:1], rhs=onesb[:, 0:1], start=True, stop=True),
                          deps=barrier_ops)
        barlist = list(bar.values())
        for b_ in (b_small, cbuf):
            pass

        def B2(name=""):
            b = Buf(name)
            b.w = list(barlist)
            return b

        off[0] = PH2_BASE
        wps = sb([4, D], BF16)
        wpc = sb([4, D], BF16)
        wo = sb([8, D], BF16)
        xnT2 = [sb([8, 512], BF16) for _ in range(2)]
        ogs = [sb([4, 512], BF16) for _ in range(2)]
        cqT = sb([4, 512], BF16)
        ckT = sb([4, 1024], BF16)
        cv = sb([8, 512], BF16)
        sZc = sb([4, 512], BF16)
        ogc2 = [sb([4, 512], BF16) for _ in range(2)]
        _o = off[0]
        BTf2 = [sb([640], F32) for _ in range(2)]
        off[0] = _o
        hT = sb([8, 512], BF16)
        Pb = [sb([2, 5, 128], BF16) for _ in range(1)] * 2
        BT = sb([8, 640], BF16)
        sq = [sb([512], BF16) for _ in range(2)]
        sg = [sb([512], F32) for _ in range(2)]
        rst = sg
        t1 = [sb([512], F32)] * 2
        t2 = [sb([512], F32)] * 2
        xres = wst2
        stg2 = [sb([512], F32) for _ in range(3)]
        rden = sb([512], F32)
        rawf = t1[0]
        knw_bc = sb([512], F32)
        small2 = sb([64], F32)

        b_wp = B2("wp")
        b_wo = B2("wo")
        b_xnT2 = [B2() for _ in range(2)]
        b_ogs = [B2() for _ in range(2)]
        b_cqT = [B2() for _ in range(4)]
        b_ckT = [[B2() for _ in range(8)] for _ in range(4)]
        b_cv = [B2() for _ in range(8)]
        b_sZc = [B2() for _ in range(4)]
        b_ogc2 = [B2(), B2()]
        b_hT = B2()
        b_Pb = [B2()] * 2
        b_BT = B2()
        b_BTh = [B2() for _ in range(8)]
        b_sq = [B2() for _ in range(2)]
        b_sg = [B2() for _ in range(2)]
        b_rst = b_sg
        b_t1 = [B2()] * 2
        b_t2 = [B2()] * 2
        b_stg2 = [B2() for _ in range(3)]
        b_xres = b_wst2
        b_rden = B2()
        b_rawf = b_t1[0]
        b_knw = B2()
        b_BTf2 = [b_hT, b_hT]
        b_small2 = B2()
        Sp = psum[:, 0:1280].rearrange("p (h j q) -> p h j q", h=2, j=5)
        EB5 = BT.rearrange("p h (j q) -> p h j q", j=5)
        OB = bank(3).rearrange("p (m q) -> p m q", m=4)
        DB = bank(4).rearrange("p (m q) -> p m q", m=4)
        MS = bank(5)
        b_S = B2()
        b_OB = B2()
        b_DB = B2()
        b_MS = B2()
        b_PJ2 = [B2(), B2()]
        preloaded = {}

        def issue_loads(I):
            xs = nx("x", 2)
            tok0 = blk_tok0(I)
            N = blk_ntok(I)
            P.add("sp", lambda q: q.dma_start(out=xnT2[xs][:, :, 0:N], in_=xnT_d[:, :, tok0:tok0 + N]),
                  reads=[xnTd_buf[I]], writes=[b_xnT2[xs]], key="xn%d" % xs)
            P.add("sp", lambda q: q.dma_start(out=ogs[xs][:, :, 0:N], in_=og_d[:, :, tok0:tok0 + N]),
                  reads=[ogd_buf[I]], writes=[b_ogs[xs]], key="ogl%d" % xs)
            return xs

        preloaded[8] = issue_loads(8)
        for h in range(8):
            P.add("sp", lambda q, h=h: q.dma_start(out=knw_bc[:, h * 64:(h + 1) * 64], in_=bass.AP(knw_d.tensor, 0, [[0, 128], [1, 64]])),
                  writes=[b_knw], key="c1")
        P.add("sp", lambda q: q.dma_start(out=BT, in_=EB_d), reads=[b_EBd], writes=b_BTh, key="ebl")

        CQ, CK, CV, CZ, GS, GC = 0, 512, 1024, 1536, 2048, 3072

        def phase2_block(I, pos, filler):
            slot = I % 2
            ogc = ogc2[pos % 2]
            b_ogc = b_ogc2[pos % 2]
            tok0 = blk_tok0(I)
            N = blk_ntok(I)
            ntile = (N + 127) // 128
            sample = (I == 8)
            want_out = sample or I == 7
            if I in preloaded:
                xs = preloaded[I]
            else:
                xs = issue_loads(I)
            xn = xnT2[xs]
            bxn = b_xnT2[xs]

            def fm(c0, pj):
                for kc in range(8):
                    P.add("pe", lambda t_, kc=kc: t_.matmul(PJ[pj][:, 0:N], lhsT=w2[:, kc, c0:c0 + 128], rhs=xn[:, kc, 0:N],
                                                              start=(kc == 0), stop=(kc == 7)),
                          reads=[b_w2, bxn], writes=[b_PJ2[pj]] if kc == 0 else [])
                b_PJ2[pj].w = [P.ops["pe"][-1]]

            if sample:
                kslot0 = 4
                for kt in range(4):
                    i = nx("wst", 2)
                    P.add("sp", lambda q, i=i, kt=kt: q.dma_start(out=wst2[i][:, 0:512], in_=cck[kt * 128:(kt + 1) * 128, :]),
                          writes=[b_wst2[i]], key="wst%d" % i)
                    sqi = nx("sq", 2)
                    P.add("dve", lambda v, i=i, sqi=sqi: v.tensor_copy(out=sq[sqi], in_=wst2[i][:, 0:512]),
                          reads=[b_wst2[i]], writes=[b_sq[sqi]])
                    pj = nx("pj", 2)
                    ptv = PTb if pj == 0 else PTb1
                    for hp in range(4):
                        P.add("pe", lambda t, hp=hp, sqi=sqi, ptv=ptv: t.transpose(out=ptv[:, hp, :], in_=sq[sqi][:, hp * 128:(hp + 1) * 128],
                                                                                    identity=ident),
                              reads=[b_sq[sqi], cbuf], writes=[b_PJ2[pj]] if hp == 0 else [])
                    b_PJ2[pj].w = [P.ops["pe"][-1]]
                    P.add("dve", lambda v, kt=kt, ptv=ptv: v.tensor_copy(out=ckT[:, :, kt * 128:(kt + 1) * 128], in_=ptv[:, 0:4, :]),
                          reads=[b_PJ2[pj]], writes=[b_ckT[hp][kt] for hp in range(4)])
                    i2 = nx("wst", 2)
                    P.add("sp", lambda q, i2=i2, kt=kt: q.dma_start(out=wst2[i2][:, 0:512], in_=ccv[kt * 128:(kt + 1) * 128, :]),
                          writes=[b_wst2[i2]], key="wst%d" % i2)
                    P.add("pool", lambda g, i2=i2, kt=kt: g.tensor_copy(out=cv[:, kt, :], in_=wst2[i2][:, 0:512]),
                          reads=[b_wst2[i2]], writes=[b_cv[kt]])
                P.add("pool", lambda g: g.memset(ckT[:, :, 512:640], 0.0), writes=[b_ckT[hp][4] for hp in range(4)])
                P.add("pool", lambda g: g.memset(cv[:, 4, :], 0.0), writes=[b_cv[4]])
            else:
                kslot0 = slot * 4

            for which in range(2):
                for hp in range(4):
                    pj = nx("pj", 2)
                    fm((CQ if which == 0 else CK) + hp * 128, pj)
                    sqi = nx("sq", 2)
                    P.add("act", lambda a, pj=pj, sqi=sqi: a.activation(out=sq[sqi][:, 0:N], in_=PJ[pj][:, 0:N], func=AF.Square),
                          reads=[b_PJ2[pj]], writes=[b_sq[sqi]])
                    P.add("pe", lambda t, sqi=sqi: t.matmul(MS[:, 0:N], lhsT=blk64, rhs=sq[sqi][:, 0:N], start=True, stop=True),
                          reads=[b_sq[sqi], cbuf], writes=[b_MS])
                    ri = nx("sg", 2)
                    P.add("act", lambda a, ri=ri: a.activation(out=rst[ri][:, 0:N], in_=MS[:, 0:N], func=AF.Ln, bias=EPS),
                          reads=[b_MS], writes=[b_rst[ri]])
                    P.add("act", lambda a, ri=ri: a.activation(out=rst[ri][:, 0:N], in_=rst[ri][:, 0:N], func=AF.Exp, scale=-0.5),
                          reads=[b_rst[ri]], writes=[b_rst[ri]])
                    if which == 0:
                        P.add("dve", lambda v, pj=pj, ri=ri, hp=hp: v.scalar_tensor_tensor(out=cqT[:, hp, 0:N], in0=PJ[pj][:, 0:N], scalar=qnw8,
                                                                                         in1=rst[ri][:, 0:N], op0=ALU.mult, op1=ALU.mult),
                              reads=[b_PJ2[pj], b_rst[ri], cbuf], writes=[b_cqT[hp]])
                    else:
                        kc0 = kslot0 * 128
                        wl = [b_ckT[hp][kslot0 + j] for j in range(ntile)]
                        P.add("dve", lambda v, pj=pj, ri=ri, hp=hp, kc0=kc0: v.scalar_tensor_tensor(out=ckT[:, hp, kc0:kc0 + N], in0=PJ[pj][:, 0:N],
                                                                                                  scalar=knw1, in1=rst[ri][:, 0:N],
                                                                                                  op0=ALU.mult, op1=ALU.mult),
                              reads=[b_PJ2[pj], b_rst[ri], cbuf], writes=wl)
            for hp in range(4):
                pj = nx("pj", 2)
                fm(CZ + hp * 128, pj)
                gi = nx("sg", 2)
                P.add("act", lambda a, pj=pj, gi=gi: a.activation(out=sg[gi][:, 0:N], in_=PJ[pj][:, 0:N], func=AF.Exp, scale=-1.0),
                      reads=[b_PJ2[pj]], writes=[b_sg[gi]])
                P.add("act", lambda a, gi=gi: a.activation(out=sg[gi][:, 0:N], in_=sg[gi][:, 0:N], func=AF.Ln, bias=1.0),
                      reads=[b_sg[gi]], writes=[b_sg[gi]])
                P.add("act", lambda a, gi=gi: a.activation(out=sg[gi][:, 0:N], in_=sg[gi][:, 0:N], func=AF.Exp, scale=-1.0),
                      reads=[b_sg[gi]], writes=[b_sg[gi]])
                P.add("dve", lambda v, pj=pj, gi=gi, hp=hp: v.tensor_tensor(out=sZc[:, hp, 0:N], in0=PJ[pj][:, 0:N], in1=sg[gi][:, 0:N], op=ALU.mult),
                      reads=[b_PJ2[pj], b_sg[gi]], writes=[b_sZc[hp]])
            for t in range(ntile):
                R = min(128, N - t * 128)
                pj = nx("pj", 2)
                for kc in range(8):
                    P.add("pe", lambda t_, kc=kc, t=t, R=R, pj=pj: t_.matmul(PJ[pj][0:R, :], lhsT=xn[:, kc, t * 128:t * 128 + R],
                                                                            rhs=w2[:, kc, CV:CV + 512], start=(kc == 0), stop=(kc == 7)),
                          reads=[b_w2, bxn], writes=[b_PJ2[pj]] if kc == 0 else [])
                b_PJ2[pj].w = [P.ops["pe"][-1]]
                vt = kslot0 + t
                P.add("dve", lambda v, R=R, vt=vt, pj=pj: v.tensor_copy(out=cv[0:R, vt, :], in_=PJ[pj][0:R, :]),
                      reads=[b_PJ2[pj]], writes=[b_cv[vt]])
                if want_out:
                    si = nx("stg", 3)
                    P.add("dve", lambda v, R=R, si=si, pj=pj: v.tensor_copy(out=stg2[si][0:R, :], in_=PJ[pj][0:R, :]),
                          reads=[b_PJ2[pj]], writes=[b_stg2[si]])
                    dst = cbv_s[0:R, :] if sample else cbv_p[t * 128:t * 128 + R, :]
                    P.add("sp", lambda q, si=si, R=R, dst=dst: q.dma_start(out=dst, in_=stg2[si][0:R, :]), reads=[b_stg2[si]], key="st%d" % si)
                    pj = nx("pj", 2)
                    for kc in range(8):
                        P.add("pe", lambda t_, kc=kc, t=t, R=R, pj=pj: t_.matmul(PJ[pj][0:R, :], lhsT=xn[:, kc, t * 128:t * 128 + R],
                                                                                rhs=w2[:, kc, CK:CK + 512], start=(kc == 0), stop=(kc == 7)),
                              reads=[b_w2, bxn], writes=[b_PJ2[pj]] if kc == 0 else [])
                    b_PJ2[pj].w = [P.ops["pe"][-1]]
                    P.add("dve", lambda v, R=R, pj=pj: v.tensor_copy(out=rawf[0:R, :], in_=PJ[pj][0:R, :]), reads=[b_PJ2[pj]], writes=[b_rawf])
                    si = nx("stg", 3)
                    P.add("dve", lambda v, R=R, si=si: v.tensor_tensor(out=stg2[si][0:R, :], in0=rawf[0:R, :], in1=rawf[0:R, :], op=ALU.mult),
                          reads=[b_rawf], writes=[b_stg2[si]])
                    ssq = small2[0:R, 0:8]
                    P.add("dve", lambda v, R=R, si=si, ssq=ssq: v.tensor_reduce(out=ssq, in_=stg2[si][0:R, :].rearrange("p (h d) -> p h d", h=8),
                                                                                axis=AX.X, op=ALU.add),
                          reads=[b_stg2[si]], writes=[b_small2])
                    P.add("dve", lambda v, ssq=ssq: v.tensor_scalar(ssq, ssq, 1.0 / 64, EPS, ALU.mult, ALU.add), reads=[b_small2], writes=[b_small2])
                    P.add("act", lambda a, ssq=ssq: a.activation(out=ssq, in_=ssq, func=AF.Ln), reads=[b_small2], writes=[b_small2])
                    P.add("act", lambda a, ssq=ssq: a.activation(out=ssq, in_=ssq, func=AF.Exp, scale=-0.5), reads=[b_small2], writes=[b_small2])
                    for h in range(8):
                        P.add("dve", lambda v, h=h, R=R, si=si: v.scalar_tensor_tensor(out=stg2[si][0:R, h * 64:(h + 1) * 64], in0=rawf[0:R, h * 64:(h + 1) * 64],
                                                                                     scalar=small2[0:R, h:h + 1], in1=knw_bc[0:R, h * 64:(h + 1) * 64],
                                                                                     op0=ALU.mult, op1=ALU.mult),
                              reads=[b_rawf, b_small2, b_knw, b_stg2[si]], writes=[b_stg2[si]])
                    dst = cbk_s[0:R, :] if sample else cbk_p[t * 128:t * 128 + R, :]
                    P.add("sp", lambda q, si=si, R=R, dst=dst: q.dma_start(out=dst, in_=stg2[si][0:R, :]), reads=[b_stg2[si]], key="st%d" % si)

            nm = 1 if sample else 4
            for hp in range(4):
                for mm in range(nm):
                    Nq = 64 if sample else 128
                    if sample:
                        jl = list(range(5))
                        kslots = {j: j for j in range(5)}
                    else:
                        mg = 4 * I + mm
                        jl = [j for j in range(5) if mg - 4 + j >= 0]
                        kslots = {j: ((mg - 4 + j) // 4 % 2) * 4 + (mg - 4 + j) % 4 for j in jl}
                    jmin = jl[0]
                    first_s = True
                    njj = 5 - jmin
                    for j in jl:
                        ks = kslots[j]
                        jj = 4 - j
                        for h in range(2):
                            P.add("pe", lambda t, jj=jj, h=h, ks=ks, mm=mm, hp=hp, Nq=Nq: t.matmul(Sp[:, h, jj, 0:Nq], lhsT=ckT[64 * h:64 * h + 64, hp, ks * 128:(ks + 1) * 128],
                                                                                    rhs=cqT[64 * h:64 * h + 64, hp, mm * 128:mm * 128 + Nq], start=True, stop=True),
                                  reads=[b_ckT[hp][ks], b_cqT[hp]], writes=[b_S] if first_s else [])
                            first_s = False
                    b_S.w = [P.ops["pe"][-1]]
                    pi = nx("p", 2)
                    P.add("act", lambda a, pi=pi, njj=njj, Nq=Nq: a.activation(out=Pb[pi][:, :, 0:njj, 0:Nq], in_=Sp[:, :, 0:njj, 0:Nq], func=AF.Exp),
                          reads=[b_S], writes=[b_Pb[pi]])
                    P.add("dve", lambda v, pi=pi, njj=njj, Nq=Nq, hp=hp: v.tensor_tensor(out=Pb[pi][:, :, 0:njj, 0:Nq], in0=Pb[pi][:, :, 0:njj, 0:Nq],
                                                                                   in1=EB5[:, 2 * hp:2 * hp + 2, 0:njj, 0:Nq], op=ALU.mult),
                          reads=[b_Pb[pi], b_BTh[2 * hp], b_BTh[2 * hp + 1]], writes=[b_Pb[pi]])
                    filler(nm * 4 - (hp * nm + mm))
                    first_o = True
                    for j in jl:
                        ks = kslots[j]
                        jj = 4 - j
                        for h in range(2):
                            c0 = hp * 128 + h * 64
                            P.add("pe", lambda t, jj=jj, j=j, h=h, ks=ks, c0=c0, pi=pi, mm=mm, Nq=Nq, jmin=jmin: t.matmul(OB[64 * h:64 * h + 64, mm, 0:Nq], lhsT=cv[:, ks, c0:c0 + 64],
                                                                                            rhs=Pb[pi][:, h, jj, 0:Nq], start=(j == jmin), stop=(j == 4)),
                                  reads=[b_Pb[pi], b_cv[ks]], writes=[b_OB] if (first_o and mm == 0) else [])
                            P.add("pe", lambda t, jj=jj, j=j, h=h, pi=pi, mm=mm, Nq=Nq, jmin=jmin: t.matmul(DB[64 * h:64 * h + 64, mm, 0:Nq], lhsT=onesb[:, 0:64],
                                                                                  rhs=Pb[pi][:, h, jj, 0:Nq], start=(j == jmin), stop=(j == 4)),
                                  reads=[cbuf], writes=[b_DB] if (first_o and mm == 0) else [])
                            first_o = False
                    b_OB.w = [P.ops["pe"][-1]]
                    b_DB.w = [P.ops["pe"][-1]]
                NQ = nm * 128 if not sample else 64
                OBf = bank(3)
                DBf = bank(4)
                P.add("act", lambda a, NQ=NQ: a.activation(out=rden[:, 0:NQ], in_=DBf[:, 0:NQ], func=AF.Ln), reads=[b_DB], writes=[b_rden])
                P.add("act", lambda a, NQ=NQ: a.activation(out=rden[:, 0:NQ], in_=rden[:, 0:NQ], func=AF.Exp, scale=-1.0),
                      reads=[b_rden], writes=[b_rden])
                P.add("dve", lambda v, NQ=NQ: v.tensor_tensor(out=rden[:, 0:NQ], in0=OBf[:, 0:NQ], in1=rden[:, 0:NQ], op=ALU.mult),
                      reads=[b_OB, b_rden], writes=[b_rden])
                P.add("dve", lambda v, NQ=NQ, hp=hp: v.tensor_tensor(out=ogc[:, hp, 0:NQ], in0=rden[:, 0:NQ], in1=sZc[:, hp, 0:NQ], op=ALU.mult),
                      reads=[b_rden, b_sZc[hp]], writes=[b_ogc] if hp == 0 else [])
                if hp != 0:
                    b_ogc.w = [P.ops["dve"][-1]]

            return stageC(I, N, ntile, tok0, xs, xn, bxn, fm, ogc, b_ogc)

        def stageC(I, N, ntile, tok0, xs, xn, bxn, fm, ogc, b_ogc):
            def sigmoid_from(pj, gidx):
                P.add("act", lambda a: a.activation(out=sg[gidx][:, 0:N], in_=PJ[pj][:, 0:N], func=AF.Exp, scale=-1.0),
                      reads=[b_PJ2[pj]], writes=[b_sg[gidx]])
                P.add("act", lambda a: a.activation(out=sg[gidx][:, 0:N], in_=sg[gidx][:, 0:N], func=AF.Ln, bias=1.0),
                      reads=[b_sg[gidx]], writes=[b_sg[gidx]])
                P.add("act", lambda a: a.activation(out=sg[gidx][:, 0:N], in_=sg[gidx][:, 0:N], func=AF.Exp, scale=-1.0),
                      reads=[b_sg[gidx]], writes=[b_sg[gidx]])

            for c in range(8):
                pj = nx("pj", 2)
                fm(GS + c * 128, pj)
                g0 = nx("sg", 2)
                sigmoid_from(pj, g0)
                yield
                pj = nx("pj", 2)
                for hp in range(4):
                    P.add("pe", lambda t, hp=hp, pj=pj, c=c: t.matmul(PJ[pj][:, 0:N], lhsT=wps[:, hp, c * 128:(c + 1) * 128], rhs=ogs[xs][:, hp, 0:N],
                                                                  start=(hp == 0), stop=(hp == 3)),
                          reads=[b_wp, b_ogs[xs]], writes=[b_PJ2[pj]] if hp == 0 else [])
                b_PJ2[pj].w = [P.ops["pe"][-1]]
                ti = nx("t", 2)
                P.add("dve", lambda v, pj=pj, g0=g0, ti=ti: v.tensor_tensor(out=t1[ti][:, 0:N], in0=PJ[pj][:, 0:N], in1=sg[g0][:, 0:N], op=ALU.mult),
                      reads=[b_PJ2[pj], b_sg[g0]], writes=[b_t1[ti]])
                yield
                pj = nx("pj", 2)
                fm(GC + c * 128, pj)
                g1 = nx("sg", 2)
                sigmoid_from(pj, g1)
                yield
                pj = nx("pj", 2)
                for hp in range(4):
                    P.add("pe", lambda t, hp=hp, pj=pj, c=c: t.matmul(PJ[pj][:, 0:N], lhsT=wpc[:, hp, c * 128:(c + 1) * 128], rhs=ogc[:, hp, 0:N],
                                                                  start=(hp == 0), stop=(hp == 3)),
                          reads=[b_wp, b_ogc], writes=[b_PJ2[pj]] if hp == 0 else [])
                b_PJ2[pj].w = [P.ops["pe"][-1]]
                P.add("dve", lambda v, pj=pj, g1=g1, ti=ti: v.tensor_tensor(out=t2[ti][:, 0:N], in0=PJ[pj][:, 0:N], in1=sg[g1][:, 0:N], op=ALU.mult),
                      reads=[b_PJ2[pj], b_sg[g1]], writes=[b_t2[ti]])
                P.add("pool", lambda g, ti=ti, c=c: g.tensor_tensor(out=hT[:, c, 0:N], in0=t1[ti][:, 0:N], in1=t2[ti][:, 0:N], op=ALU.add),
                      reads=[b_t1[ti], b_t2[ti]], writes=[b_hT] if c == 0 else [])
                if c != 0:
                    b_hT.w = [P.ops["pool"][-1]]
                yield
            items = [(t, hf) for t in range(ntile) for hf in range(2)]

            def issue_x(i):
                t, hf = items[i]
                R = min(128, N - t * 128)
                xi = nx("xr", 2)
                P.add("sp", lambda q: q.dma_start(out=xres[xi][0:R, :], in_=x_rows(tok0 + t * 128, R)[:, hf * 512:(hf + 1) * 512]),
                      writes=[b_xres[xi]], key="xr%d" % xi)
                return xi

            def y_cons(i, pj, xi):
                t, hf = items[i]
                R = min(128, N - t * 128)
                si = nx("stg", 3)
                P.add("dve", lambda v: v.tensor_tensor(out=stg2[si][0:R, :], in0=PJ[pj][0:R, :], in1=xres[xi][0:R, :], op=ALU.add),
                      reads=[b_PJ2[pj], b_xres[xi]], writes=[b_stg2[si]])
                P.add("sp", lambda q: q.dma_start(out=y_rows(tok0 + t * 128, R, hf * 512, 512), in_=stg2[si][0:R, :]),
                      reads=[b_stg2[si]], key="st%d" % si)

            xis = {0: issue_x(0)}
            pend = None
            for i, (t, hf) in enumerate(items):
                R = min(128, N - t * 128)
                if pend is not None:
                    y_cons(*pend)
                if i + 1 < len(items):
                    xis[i + 1] = issue_x(i + 1)
                pj = nx("pj", 2)
                for kc in range(8):
                    P.add("pe", lambda t_, kc=kc, t=t, R=R, pj=pj, hf=hf: t_.matmul(PJ[pj][0:R, :], lhsT=hT[:, kc, t * 128:t * 128 + R],
                                                                                   rhs=wo[:, kc, hf * 512:(hf + 1) * 512], start=(kc == 0), stop=(kc == 7)),
                          reads=[b_wo, b_hT], writes=[b_PJ2[pj]] if kc == 0 else [])
                b_PJ2[pj].w = [P.ops["pe"][-1]]
                pend = (i, pj, xis[i])
                yield
            y_cons(*pend)
            yield

        state = {"gen": None, "left": 0}

        def filler(slots_left):
            g = state["gen"]
            if g is None:
                return
            k = max(1, -(-state["left"] // max(1, slots_left)))
            for _ in range(k):
                try:
                    next(g)
                    state["left"] = max(0, state["left"] - 1)
                except StopIteration:
                    state["gen"] = None
                    if state.get("on_done") is not None:
                        state["on_done"]()
                        state["on_done"] = None
                    break

        def drain():
            g = state["gen"]
            if g is not None:
                for _ in g:
                    pass
            state["gen"] = None
            if state.get("on_done") is not None:
                state["on_done"]()
                state["on_done"] = None

        preloaded[0] = issue_loads(0)
        order2 = [8] + list(range(8))
        for pos, I in enumerate(order2):
            gC = phase2_block(I, pos, filler)
            if pos == 0:
                wslots["list"] = [(wst2[0], b_wst2[0], "w2s0"), (wst2[1], b_wst2[1], "w2s1"), (stg2[0], b_stg2[0], "st0"),
                                  (stg2[1], b_stg2[1], "st1"), (stg2[2], b_stg2[2], "st2")]
                for hp in range(4):
                    for hf in range(2):
                        load_w2_chunk(wps_d[hp * 128:(hp + 1) * 128, hf * 512:(hf + 1) * 512], wps[:, hp, hf * 512:(hf + 1) * 512], b_wp)
                        load_w2_chunk(wpc_d[hp * 128:(hp + 1) * 128, hf * 512:(hf + 1) * 512], wpc[:, hp, hf * 512:(hf + 1) * 512], b_wp)
                for kc in range(8):
                    for hf in range(2):
                        load_w2_chunk(wo_d[kc * 128:(kc + 1) * 128, hf * 512:(hf + 1) * 512], wo[:, kc, hf * 512:(hf + 1) * 512], b_wo)
            drain()
            state["gen"] = gC
            state["left"] = 48
            nxt2 = order2[pos + 2] if pos + 2 < len(order2) else None
            if nxt2 is not None and nxt2 not in preloaded:
                state["on_done"] = (lambda nxt2=nxt2: preloaded.__setitem__(nxt2, issue_loads(nxt2)))
        drain()

        print('PH2 end cols', off[0], 'of', ARENA_COLS)
        assert off[0] <= ARENA_COLS
        block = es.enter_context(nc.Block())
        P.emit(block)
    return nc


_CACHE = {}


def kernel(x_prompt, x_sample, cache_sb_k, cache_sb_v, cache_cb_k, cache_cb_v,
           norm_w, w_in, q_norm_w, k_norm_w, rel_bias, w_proj_sb, w_proj_cb, w_out):
    f = lambda a: np.ascontiguousarray(np.asarray(a, dtype=np.float32))
    x_prompt, x_sample = f(x_prompt), f(x_sample)
    cache_sb_k, cache_sb_v, cache_cb_k, cache_cb_v = f(cache_sb_k), f(cache_sb_v), f(cache_cb_k), f(cache_cb_v)
    nw, wi, qn, kn, rb = f(norm_w), f(w_in), f(q_norm_w), f(k_norm_w), f(rel_bias)
    wps, wpc, wo = f(w_proj_sb), f(w_proj_cb), f(w_out)
    n = 8
    if "nc" not in _CACHE:
        _CACHE["nc"] = build_program()
    nc = _CACHE["nc"]
    in_maps = []
    for c in range(n):
        in_maps.append({
            "x_p": x_prompt[c], "x_s": x_sample[c],
            "csk": cache_sb_k[0, c].reshape(PAST, 512), "csv": cache_sb_v[0, c].reshape(PAST, 512),
            "cck": cache_cb_k[0, c].reshape(512, 512), "ccv": cache_cb_v[0, c].reshape(512, 512),
            "norm_w": nw[0:1], "w_in": wi[0], "qnw": qn[0:1], "knw": kn[0:1], "relb": rb[0],
            "wps": wps[0], "wpc": wpc[0], "wo": wo[0],
        })
    res = run_bass_kernel_spmd(nc, in_maps, core_ids=list(range(n)))
    R = res.results
    st = lambda k: np.stack([np.asarray(R[c][k], dtype=np.float32) for c in range(n)])
    y_p = st("y_p")
    y_s = st("y_s")
    sbk_p = st("sbk_p").reshape(1, n, T_P, 8, 64)
    sbv_p = st("sbv_p").reshape(1, n, T_P, 8, 64)
    cbk_p = st("cbk_p").reshape(1, n, 512, 8, 64)
    cbv_p = st("cbv_p").reshape(1, n, 512, 8, 64)
    sbk_s = st("sbk_s").reshape(1, n, T_S, 8, 64)
    sbv_s = st("sbv_s").reshape(1, n, T_S, 8, 64)
    cbk_s = st("cbk_s").reshape(1, n, T_S, 8, 64)
    cbv_s = st("cbv_s").reshape(1, n, T_S, 8, 64)
    if DEBUG:
        _CACHE["dbg"] = R
    return (y_p, y_s, sbk_p, sbv_p, cbk_p, cbv_p, sbk_s, sbv_s, cbk_s, cbv_s)
```
